# Optimizing a Trainium2 kernel written in Bass

```python
import math
import jax, jax.numpy as jnp
from jax import lax
import numpy as np

D_MODEL = 1024
BATCH = 4
SEQ = 8192
DEPTH = 1

CHUNK = 64
D_MIX = 2 * D_MODEL
D_MLSTM = D_MIX // 2
D_RGLRU = D_MIX - D_MLSTM
MLSTM_HEADS = 4
MLSTM_HEAD_DIM = D_MLSTM // MLSTM_HEADS
QKV_BLOCK = 4
RGLRU_BLOCKS = 4
RGLRU_BLOCK_W = D_RGLRU // RGLRU_BLOCKS
CONV_W = 4
RGLRU_C = 8.0
D_FF = 2816
EPS = 1e-6
M_INIT = -1e30

kernel_name = "hymba_mlstm_rglru_macaron_block"


def rmsnorm(x, g):
    xf = x.astype(jnp.float32)
    y = xf * lax.rsqrt(jnp.mean(xf * xf, axis=-1, keepdims=True) + EPS) * g.astype(jnp.float32)
    return y.astype(x.dtype)


def swiglu(x, wg, wu, wd):
    return (jax.nn.silu(x @ wg) * (x @ wu)) @ wd


def causal_dwconv(x, w, b):
    s = x.shape[1]
    xp = jnp.pad(x, ((0, 0), (CONV_W - 1, 0), (0, 0)))
    y = b
    for tap in range(CONV_W):
        y = y + w[tap] * xp[:, tap:tap + s]
    return y


def blockdiag(x, w):
    bsz, s, _ = x.shape
    nb, bi, bo = w.shape
    y = jnp.einsum('bsni,nio->bsno', x.reshape(bsz, s, nb, bi), w)
    return y.reshape(bsz, s, nb * bo)


def _mlstm_chunk_step(carry, inp):
    c_mem, n_mem, m_prev = carry
    q, k, v, ig, lf = inp
    L = q.shape[2]
    b = jnp.cumsum(lf, axis=-1)
    g = b[..., -1]
    causal = jnp.tril(jnp.ones((L, L), dtype=bool))
    dlog = b[..., :, None] - b[..., None, :] + ig[..., None, :]
    dlog = jnp.where(causal, dlog, -jnp.inf)
    inter = b + m_prev[..., None]
    m_row = jnp.maximum(inter, jnp.max(dlog, axis=-1))
    s = jnp.einsum('bhid,bhjd->bhij', q, k) * jnp.exp(dlog - m_row[..., None])
    inter_w = jnp.exp(inter - m_row)
    num = (jnp.einsum('bhij,bhjd->bhid', s, v)
           + inter_w[..., None] * jnp.einsum('bhvk,bhik->bhiv', c_mem, q))
    den = jnp.sum(s, axis=-1) + inter_w * jnp.einsum('bhk,bhik->bhi', n_mem, q)
    h = num / jnp.maximum(jnp.abs(den), jnp.exp(-m_row))[..., None]
    w = g[..., None] - b + ig
    m_new = jnp.maximum(g + m_prev, jnp.max(w, axis=-1))
    decay = jnp.exp(g + m_prev - m_new)
    wk = jnp.exp(w - m_new[..., None])
    c_new = decay[..., None, None] * c_mem + jnp.einsum('bhjv,bhjk->bhvk', v * wk[..., None], k)
    n_new = decay[..., None] * n_mem + jnp.einsum('bhj,bhjk->bhk', wk, k)
    return (c_new, n_new, m_new), h


def mlstm_group(x_m, z_m, conv_w, conv_b, wq, wk, wv, w_gates, b_gates, ln_w, skip):
    bsz, s, _ = x_m.shape
    H, dh = MLSTM_HEADS, MLSTM_HEAD_DIM
    nc = s // CHUNK
    xc = jax.nn.silu(causal_dwconv(x_m, conv_w, conv_b))
    q = blockdiag(xc, wq)
    k = blockdiag(xc, wk)
    v = blockdiag(x_m, wv)
    gates = (jnp.concatenate([q, k, v], axis=-1) @ w_gates + b_gates).astype(jnp.float32)
    ig = gates[..., :H]
    lf = jax.nn.log_sigmoid(gates[..., H:])

    def to_chunks(t):
        return t.astype(jnp.float32).reshape(bsz, nc, CHUNK, H, dh).transpose(1, 0, 3, 2, 4)

    def gate_chunks(t):
        return t.reshape(bsz, nc, CHUNK, H).transpose(1, 0, 3, 2)

    xs = (to_chunks(q), to_chunks(k) * (dh ** -0.5), to_chunks(v), gate_chunks(ig), gate_chunks(lf))
    carry0 = (jnp.zeros((bsz, H, dh, dh), jnp.float32),
              jnp.zeros((bsz, H, dh), jnp.float32),
              jnp.full((bsz, H), M_INIT, jnp.float32))
    _, hs = lax.scan(_mlstm_chunk_step, carry0, xs)
    h = hs.transpose(1, 0, 3, 2, 4).reshape(bsz, s, H, dh)
    mu = jnp.mean(h, axis=-1, keepdims=True)
    var = jnp.mean(jnp.square(h - mu), axis=-1, keepdims=True)
    hn = (h - mu) * lax.rsqrt(var + EPS) * ln_w.astype(jnp.float32).reshape(H, dh)
    hn = hn.reshape(bsz, s, D_MLSTM).astype(x_m.dtype)
    return (hn + skip * xc) * jax.nn.silu(z_m)


def _lin_combine(left, right):
    a_l, b_l = left
    a_r, b_r = right
    return a_l * a_r, a_r * b_l + b_r


def rglru_group(x_r, y_r, conv_w, conv_b, w_a, b_a, w_x, b_x, lam):
    xc = causal_dwconv(x_r, conv_w, conv_b)
    r = jax.nn.sigmoid((blockdiag(xc, w_a) + b_a).astype(jnp.float32))
    i = jax.nn.sigmoid((blockdiag(xc, w_x) + b_x).astype(jnp.float32))
    log_a = RGLRU_C * r * jax.nn.log_sigmoid(lam.astype(jnp.float32))
    a = jnp.exp(log_a)
    u = jnp.sqrt(-jnp.expm1(2.0 * log_a)) * (i * xc.astype(jnp.float32))
    _, h = lax.associative_scan(_lin_combine, (a, u), axis=1)
    return h.astype(x_r.dtype) * jax.nn.gelu(y_r)


def setup_inputs(seed: int = 0) -> dict:
    key = jax.random.key(seed)
    ks = iter(jax.random.split(key, 48))
    Ly = DEPTH
    H = MLSTM_HEADS

    def nrm(shape, scale):
        return jax.random.normal(next(ks), shape, jnp.float32) * scale

    def gain(shape):
        return 1.0 + nrm(shape, 0.02)

    x = nrm((BATCH, SEQ, D_MODEL), 1.0)
    norm_ffn1 = gain((Ly, D_MODEL))
    ffn1_wg = nrm((Ly, D_MODEL, D_FF), D_MODEL ** -0.5)
    ffn1_wu = nrm((Ly, D_MODEL, D_FF), D_MODEL ** -0.5)
    ffn1_wd = nrm((Ly, D_FF, D_MODEL), D_FF ** -0.5)
    norm_mix = gain((Ly, D_MODEL))
    w_in = nrm((Ly, D_MODEL, 2 * D_MLSTM + 2 * D_RGLRU), D_MODEL ** -0.5)
    m_conv_w = nrm((Ly, CONV_W, D_MLSTM), CONV_W ** -0.5)
    m_conv_b = nrm((Ly, D_MLSTM), 0.02)
    nqb = D_MLSTM // QKV_BLOCK
    m_wq = nrm((Ly, nqb, QKV_BLOCK, QKV_BLOCK), QKV_BLOCK ** -0.5)
    m_wk = nrm((Ly, nqb, QKV_BLOCK, QKV_BLOCK), QKV_BLOCK ** -0.5)
    m_wv = nrm((Ly, nqb, QKV_BLOCK, QKV_BLOCK), QKV_BLOCK ** -0.5)
    m_w_gates = nrm((Ly, 3 * D_MLSTM, 2 * H), (3 * D_MLSTM) ** -0.5)
    i_bias = nrm((Ly, H), 0.1)
    f_bias = jnp.linspace(3.0, 6.0, H, dtype=jnp.float32) + nrm((Ly, H), 0.1)
    m_b_gates = jnp.concatenate([i_bias, f_bias], axis=-1)
    m_ln_w = gain((Ly, D_MLSTM))
    m_skip = gain((Ly, D_MLSTM))
    r_conv_w = nrm((Ly, CONV_W, D_RGLRU), CONV_W ** -0.5)
    r_conv_b = nrm((Ly, D_RGLRU), 0.02)
    r_w_a = nrm((Ly, RGLRU_BLOCKS, RGLRU_BLOCK_W, RGLRU_BLOCK_W), RGLRU_BLOCK_W ** -0.5)
    r_b_a = nrm((Ly, D_RGLRU), 0.02)
    r_w_x = nrm((Ly, RGLRU_BLOCKS, RGLRU_BLOCK_W, RGLRU_BLOCK_W), RGLRU_BLOCK_W ** -0.5)
    r_b_x = nrm((Ly, D_RGLRU), 0.02)
    a0 = jax.random.uniform(next(ks), (Ly, D_RGLRU), jnp.float32, minval=0.9, maxval=0.999)
    sa = a0 ** (1.0 / RGLRU_C)
    r_lam = jnp.log(sa) - jnp.log1p(-sa)
    out_norm_m = gain((Ly, D_MLSTM))
    out_norm_r = gain((Ly, D_RGLRU))
    w_out = nrm((Ly, D_MIX, D_MODEL), D_MIX ** -0.5)
    norm_ffn2 = gain((Ly, D_MODEL))
    ffn2_wg = nrm((Ly, D_MODEL, D_FF), D_MODEL ** -0.5)
    ffn2_wu = nrm((Ly, D_MODEL, D_FF), D_MODEL ** -0.5)
    ffn2_wd = nrm((Ly, D_FF, D_MODEL), D_FF ** -0.5)
    norm_final = gain((D_MODEL,))
    return {"x": x, "norm_ffn1": norm_ffn1, "ffn1_wg": ffn1_wg, "ffn1_wu": ffn1_wu,
            "ffn1_wd": ffn1_wd, "norm_mix": norm_mix, "w_in": w_in,
            "m_conv_w": m_conv_w, "m_conv_b": m_conv_b, "m_wq": m_wq, "m_wk": m_wk,
            "m_wv": m_wv, "m_w_gates": m_w_gates, "m_b_gates": m_b_gates,
            "m_ln_w": m_ln_w, "m_skip": m_skip, "r_conv_w": r_conv_w,
            "r_conv_b": r_conv_b, "r_w_a": r_w_a, "r_b_a": r_b_a, "r_w_x": r_w_x,
            "r_b_x": r_b_x, "r_lam": r_lam, "out_norm_m": out_norm_m,
            "out_norm_r": out_norm_r, "w_out": w_out, "norm_ffn2": norm_ffn2,
            "ffn2_wg": ffn2_wg, "ffn2_wu": ffn2_wu, "ffn2_wd": ffn2_wd,
            "norm_final": norm_final}


def reference(x, norm_ffn1, ffn1_wg, ffn1_wu, ffn1_wd, norm_mix, w_in, m_conv_w, m_conv_b,
              m_wq, m_wk, m_wv, m_w_gates, m_b_gates, m_ln_w, m_skip, r_conv_w, r_conv_b,
              r_w_a, r_b_a, r_w_x, r_b_x, r_lam, out_norm_m, out_norm_r, w_out, norm_ffn2,
              ffn2_wg, ffn2_wu, ffn2_wd, norm_final):
    split_at = [D_MLSTM, 2 * D_MLSTM, 2 * D_MLSTM + D_RGLRU]
    for l in range(DEPTH):
        x = x + 0.5 * swiglu(rmsnorm(x, norm_ffn1[l]), ffn1_wg[l], ffn1_wu[l], ffn1_wd[l])
        h = rmsnorm(x, norm_mix[l])
        proj = h @ w_in[l]
        x_m, z_m, x_r, y_r = jnp.split(proj, split_at, axis=-1)
        out_m = mlstm_group(x_m, z_m, m_conv_w[l], m_conv_b[l], m_wq[l], m_wk[l], m_wv[l],
                            m_w_gates[l], m_b_gates[l], m_ln_w[l], m_skip[l]).astype(x.dtype)
        out_r = rglru_group(x_r, y_r, r_conv_w[l], r_conv_b[l], r_w_a[l], r_b_a[l],
                            r_w_x[l], r_b_x[l], r_lam[l]).astype(x.dtype)
        mixed = jnp.concatenate([rmsnorm(out_m, out_norm_m[l]),
                                 rmsnorm(out_r, out_norm_r[l])], axis=-1)
        x = x + mixed @ w_out[l]
        x = x + 0.5 * swiglu(rmsnorm(x, norm_ffn2[l]), ffn2_wg[l], ffn2_wu[l], ffn2_wd[l])
    return rmsnorm(x, norm_final)
```

```python
import numpy as np
from contextlib import ExitStack
import concourse.bass as bass
import concourse.mybir as mybir
from concourse.bass_utils import run_bass_kernel_spmd

F32 = mybir.dt.float32
BF16 = mybir.dt.bfloat16
AF = mybir.ActivationFunctionType
ALU = mybir.AluOpType

P = 128
T = 512
NSUB = 4
NFF = 22
D_MODEL = 1024
D_FF = 2816
EPS = 1e-6
NS = 4
SLOT = 4096
GELU_C = 1.5957691216057308

COLS = {}
_o = 0
for _n, _w in [("g1", 8), ("gmix", 8), ("g2", 8), ("gfin", 8), ("mcw", 32), ("mcb", 8), ("rcw", 32), ("rcb", 8),
               ("rba", 8), ("rbx", 8), ("rlam", 8), ("mln", 8), ("mskip", 8), ("gom", 8), ("gor", 8), ("flag", 1)]:
    COLS[_n] = _o
    _o += _w
NCOL = _o


class Sched:
    ENGS = ('pe', 'act', 'dve', 'pool', 'sp')

    def __init__(self, nc, stack, dry=False):
        self.nc = nc
        self.stack = stack
        self.dry = dry
        self.lists = {e: [] for e in self.ENGS}
        self.sems = {}
        self.cnt = {}
        self.known = {e: {} for e in self.ENGS}
        self.lastw = {}
        self.readers = {}
        self.nins = {e: 0 for e in self.ENGS}
        self.evs = {}
        self.tag = "setup"
        self.pe_tags = []
        for e in self.ENGS:
            self._mk(e)

    def fix_total(self, semname):
        for ev in self.evs.get(semname, []):
            ev[1] = self.cnt[semname]

    def _mk(self, name):
        self.sems[name] = None if self.dry else self.stack.enter_context(self.nc.semaphore(name))
        self.cnt[name] = 0

    def _wait(self, eng, ev):
        if ev is None:
            return
        s, v = ev
        if eng == 'pe' and s == 'pe':
            return
        if self.known[eng].get(s, 0) >= v:
            return
        self.known[eng][s] = v
        sem = self.sems[s]
        self.lists[eng].append(lambda E, sem=sem, v=v: E.wait_ge(sem, v))

    def _deps(self, eng, reads, writes):
        for k in reads:
            self._wait(eng, self.lastw.get(k))
        for k in writes:
            self._wait(eng, self.lastw.get(k))
            for ev in self.readers.get(k, ()):
                self._wait(eng, ev)

    def _commit(self, ev, reads, writes):
        for k in reads:
            self.readers.setdefault(k, []).append(ev)
        for k in writes:
            self.lastw[k] = ev
            self.readers[k] = []

    def op(self, eng, thunks, reads=(), writes=()):
        if callable(thunks):
            thunks = [thunks]
        self._deps(eng, reads, writes)
        self.cnt[eng] += 1
        ev = (eng, self.cnt[eng])
        sem = self.sems[eng]
        n = len(thunks)
        self.nins[eng] += n
        if eng == 'pe':
            self.pe_tags += [self.tag] * n
        for i, th in enumerate(thunks):
            if i == n - 1:
                self.lists[eng].append(lambda E, th=th, sem=sem: th(E).then_inc(sem, 1))
            else:
                self.lists[eng].append(th)
        self._commit(ev, reads, writes)
        return ev

    def dma(self, q, out, in_, semname, reads=(), writes=()):
        if semname not in self.sems:
            self._mk(semname)
        self._deps(q, reads, writes)
        self.cnt[semname] += 16
        ev = [semname, self.cnt[semname]]
        self.evs.setdefault(semname, []).append(ev)
        sem = self.sems[semname]
        self.lists[q].append(lambda E, out=out, in_=in_, sem=sem: E.dma_start(out=out, in_=in_).then_inc(sem, 16))
        self._commit(ev, reads, writes)
        return ev

    def emit(self):
        nc = self.nc
        L = self.lists
        with nc.Block() as block:
            @block.tensor
            def _(E):
                for th in L['pe']:
                    th(E)

            @block.scalar
            def _(E):
                for th in L['act']:
                    th(E)

            @block.vector
            def _(E):
                for th in L['dve']:
                    th(E)

            @block.gpsimd
            def _(E):
                for th in L['pool']:
                    th(E)

            @block.sync
            def _(E):
                for th in L['sp']:
                    th(E)


def build_program(n_pre, n_main, ntok_pre, ntok_main, dbg=False):
    nc = bass.Bass("TRN2", target_bir_lowering=False)
    dr = lambda name, shape, dt=F32, kind="ExternalInput": nc.dram_tensor(name, list(shape), dt, kind=kind).ap()
    d_xT = dr("xT", [D_MODEL, ntok_main])
    d_xpT = dr("xpT", [D_MODEL, max(ntok_pre, T)])
    d_out = dr("outT", [D_MODEL, ntok_main], kind="ExternalOutput")
    d_cols = dr("cols", [P, NCOL])
    d_bg = dr("bgates", [P, NSUB * 8])
    d_wqkv = dr("wqkv", [P, 3 * 8 * 128])
    d_wrg = dr("wrg", [P, 2 * 4 * 2 * 256])
    d_wgt = dr("wgt", [P, 24 * 8])
    wshapes = {"gu1": (11, SLOT), "d1": (8, NFF * 128), "win": (8, SLOT), "woutm": (2, SLOT), "woutr": (2, SLOT),
               "gu2": (11, SLOT), "d2": (8, NFF * 128)}
    d_w2 = {k: dr("w_" + k, [n * P + 1, L]) for k, (n, L) in wshapes.items()}
    d_w = {k: [d_w2[k][u * P:(u + 1) * P, :] for u in range(n)] for k, (n, L) in wshapes.items()}
    d_ws = {k: dr("ws_" + k, [n, P, L], BF16, kind="Internal") for k, (n, L) in wshapes.items()}
    dbg_out = {}

    with ExitStack() as st:
        sb = lambda name, shape, dt: st.enter_context(nc.sbuf_tensor(name, list(shape), dt))
        XB = sb("X", [P, 2, 8, T], F32)
        cur = {"p": 0}
        XN = sb("XN", [P, 8, T], BF16)
        SQ = sb("SQ", [P, 2, T], BF16)
        RB = sb("RB", [P, T], F32)
        HID = sb("HID", [P, 24, T], BF16)
        HIDF = HID.bitcast(F32)
        WR = sb("WR", [P, NS, SLOT], BF16)
        GT = sb("GT", [P, 10, T], F32)
        XH = sb("XH", [P, 2, T + 3], F32)
        HS = GT[:, 8:10, :].rearrange("p a b -> p (a b)")
        HN = GT[:, 6:8, :].rearrange("p a b -> p (a b)")
        XNF = sb("XNF", [P, 8, T], BF16)
        B2B = None
        HALO = sb("HALO", [P, 2, 8, 3], F32)
        B2 = sb("B2", [P, 8, T], F32)
        MIX = sb("MIX", [P, 16, T], BF16)
        B2B = B2.bitcast(BF16)
        ST = sb("ST", [P, 4, 2, 260], F32)
        STB = sb("STB", [P, 4, 2, 260], BF16)
        KTOK = sb("KTOK", [P, 2, 1024], BF16)
        VTOK = sb("VTOK", [P, 2, 4, 260], BF16)
        NUMS = sb("NUMS", [P, 4, 260], F32)
        NUMSF = NUMS[:].rearrange("p a b -> p (a b)")
        KTOKF = KTOK.bitcast(F32)
        PT = sb("PT", [P, 4, 128], BF16)
        G = sb("G", [P, NSUB, 8], F32)
        E1 = sb("E1", [P, NSUB, 4], F32)
        RR = sb("RR", [P, NSUB, 4], F32)
        CC = sb("CC", [P, NSUB, 4], F32)
        EG = sb("EG", [P, NSUB, 4], F32)
        SM = sb("SM", [P, 8, 4], F32)
        BNS = sb("BNS", [P, 4, 6], F32)
        MV = sb("MV", [P, 4, 2], F32)
        HST = sb("HST", [P, 8], F32)
        CL = sb("CL", [P, NCOL], F32)
        CD = sb("CD", [P, 32], F32)
        BG = sb("BG", [P, NSUB, 8], F32)
        WQKV = sb("WQKV", [P, 3, 8, 128], BF16)
        WRG = sb("WRG", [P, 2, 4, 2, 256], BF16)
        WGT = sb("WGT", [P, 24, 8], BF16)
        IDENT = sb("IDENT", [P, P], F32)
        UF = sb("UF", [P, P], F32)
        ONESF = sb("ONESF", [P, P], F32)
        ONESB = sb("ONESB", [P, P], BF16)
        PSB = [st.enter_context(nc.psum_tensor("ps%d" % b, [P, T], F32)) for b in range(8)]


        def body(order_in):
            dry = order_in is None
            S = Sched(nc, st, dry=dry)
            col = lambda name, i=0: CL[:, COLS[name] + i:COLS[name] + i + 1]
            dumps = []

            def dump(name, ap, shape, keys):
                if not dbg or dry or name in dumps:
                    return
                dumps.append(name)
                d = nc.dram_tensor("dbg_" + name, list(shape), F32, kind="ExternalOutput").ap()
                S.dma('sp', d, ap, 'dbg_' + name, reads=keys)

            pstate = {"b": 0, "qb": None, "q": 4}

            def ps_full(hold=False):
                b = pstate["b"]
                while b in pstate.setdefault("held", set()):
                    b = (b + 1) % 8
                pstate["b"] = (b + 1) % 8
                if hold:
                    pstate["held"].add(b)
                return PSB[b], [("ps", b)]

            def ps_release(keys):
                pstate["held"].discard(keys[0][1])

            def ps_q():
                t_, k_ = ps_full()
                return t_[:, 0:128], k_

            def act(out, in_, func, reads, writes, scale=1.0, bias=0.0, accum=None, eng='act'):
                if accum is None:
                    S.op('act', lambda E: E.activation(out=out, in_=in_, func=func, bias=bias, scale=scale),
                         reads, writes)
                else:
                    S.op('act', lambda E: E.activation(out=out, in_=in_, func=func, bias=bias, scale=scale,
                                                       accum_out=accum), reads, writes)

            def tt(eng, out, a, b, op, reads, writes):
                S.op(eng, lambda E: E.tensor_tensor(out=out, in0=a, in1=b, op=op), reads, writes)

            def ts(eng, out, a, s1, s2, op0, op1, reads, writes):
                if s2 is None:
                    S.op(eng, lambda E: E.tensor_scalar(out=out, in0=a, scalar1=s1, scalar2=None, op0=op0), reads, writes)
                else:
                    S.op(eng, lambda E: E.tensor_scalar(out=out, in0=a, scalar1=s1, scalar2=s2, op0=op0, op1=op1),
                         reads, writes)

            def stt(out, a, scalar, b, op0, op1, reads, writes):
                S.op('dve', lambda E: E.scalar_tensor_tensor(out=out, in0=a, scalar=scalar, in1=b, op0=op0, op1=op1),
                     reads, writes)

            def cp(eng, out, in_, reads, writes):
                if eng == 'act':
                    S.op('act', lambda E: E.activation(out=out, in_=in_, func=AF.Copy), reads, writes)
                else:
                    S.op(eng, lambda E: E.tensor_copy(out=out, in_=in_), reads, writes)

            def mm(out, pairs, reads, writes):
                n = len(pairs)
                ths = []
                for i, (l, r) in enumerate(pairs):
                    ths.append(lambda E, l=l, r=r, i=i: E.matmul(out, lhsT=l, rhs=r, start=(i == 0), stop=(i == n - 1)))
                S.op('pe', ths, reads, writes)

            def mm_seq(out, pairs, per_reads, common, writes):
                n = len(pairs)
                for i, ((l, r), rk) in enumerate(zip(pairs, per_reads)):
                    S.op('pe', lambda E, l=l, r=r, i=i: E.matmul(out, lhsT=l, rhs=r, start=(i == 0), stop=(i == n - 1)),
                         list(rk) + list(common), writes)

            def sigmoid_recip(dst, src, reads_src, kdst, scale=-1.0, bias=0.0):
                act(dst, src, AF.Exp, reads_src, kdst, scale=scale, bias=bias)
                ts('dve', dst, dst, 1.0, None, ALU.add, None, kdst, kdst)
                S.op('dve', lambda E: E.reciprocal(out=dst, in_=dst), kdst, kdst)

            def sigmoid_chain(dst, src, reads_src, kdst, scale=-1.0, bias=0.0):
                act(dst, src, AF.Exp, reads_src, kdst, scale=scale, bias=bias)
                act(dst, dst, AF.Ln, kdst, kdst, scale=1.0, bias=1.0)
                act(dst, dst, AF.Exp, kdst, kdst, scale=-1.0)

            S.dma('sp', CL[:], d_cols, 'ld0', writes=['CL'])
            S.dma('sp', BG[:].rearrange("p a b -> p (a b)"), d_bg, 'ld0', writes=['BG'])
            S.dma('pool', WQKV[:].rearrange("p a b c -> p (a b c)"), d_wqkv, 'ldp', writes=['WQKV'])
            S.dma('pool', WRG[:].rearrange("p a b c d -> p (a b c d)"), d_wrg, 'ldp', writes=['WRG'])
            S.dma('pool', WGT[:].rearrange("p a b -> p (a b)"), d_wgt, 'ldp', writes=['WGT'])
            S.fix_total('ld0')
            S.fix_total('ldp')
            NCAST = 6
            cst = {"i": 0}
            early = [("gu1", u) for u in range(11)] + [("d1", u) for u in range(8)] + [("win", u) for u in (0, 1, 4, 5)]
            late = ([("win", 2), ("win", 3), ("woutm", 0), ("woutm", 1), ("win", 6), ("win", 7), ("woutr", 0), ("woutr", 1)]
                    + [("gu2", u) for u in range(11)] + [("d2", u) for u in range(8)])
            if n_pre == 0:
                early, late = early + late, []

            def emit_casts(lst, reads=()):
                for k, u in lst:
                    ci = cst["i"]
                    S.dma('pool', d_ws[k][u], d_w[k][u], 'cast%d' % (ci % NCAST), reads=list(reads),
                          writes=[("ws", k, u), ("castslot", ci % NCAST)])
                    cst["i"] += 1

            emit_casts(early)
            S.op('pool', lambda E: E.memset(ONESF[:], 1.0), writes=['ONESF'])
            S.op('pool', lambda E: E.memset(ONESB[:], 1.0), writes=['ONESB'])
            S.op('pool', lambda E: E.memset(IDENT[:], 1.0), writes=['IDENT'])
            S.op('pool', lambda E: E.affine_select(out=IDENT[:], in_=IDENT[:], pattern=[[-1, P]], compare_op=ALU.is_equal,
                                                   fill=0.0, base=0, channel_multiplier=1), ['IDENT'], ['IDENT'])
            S.op('pool', lambda E: E.memset(UF[:], 1.0), writes=['UF'])
            S.op('pool', lambda E: E.affine_select(out=UF[:], in_=UF[:], pattern=[[1, P]], compare_op=ALU.is_ge,
                                                   fill=0.0, base=0, channel_multiplier=-1), ['UF'], ['UF'])
            S.op('pool', lambda E: E.memset(ST[:].rearrange("p a b c -> p (a b c)"), 0.0), writes=[("ST", h, e) for h in range(4) for e in range(2)])
            S.op('pool', lambda E: E.memset(STB[:].rearrange("p a b c -> p (a b c)"), 0.0), writes=[("STB", h, e) for h in range(4) for e in range(2)])
            S.op('pool', lambda E: E.memset(HALO[:].rearrange("p a b c -> p (a b c)"), 0.0), writes=[("HALO", k, c) for k in range(2) for c in range(8)])
            S.op('pool', lambda E: E.memset(HST[:], 0.0), writes=[("HST", c) for c in range(8)])
            S.op('pool', lambda E: E.memset(VTOK[:].rearrange("p a b c -> p (a b c)"), 0.0), writes=['VTOK0', 'VTOK1'])
            lam = CL[:, COLS["rlam"]:COLS["rlam"] + 8]
            act(CD[:, 0:8], lam, AF.Exp, ['CL'], ['CD'], scale=-1.0)
            act(CD[:, 0:8], CD[:, 0:8], AF.Ln, ['CD'], ['CD'], bias=1.0)
            ts('dve', CD[:, 8:16], CD[:, 0:8], -16.0, None, ALU.mult, None, ['CD'], ['CD'])
            ts('dve', CD[:, 0:8], CD[:, 0:8], -8.0, None, ALU.mult, None, ['CD'], ['CD'])
            ts('dve', CD[:, 16:24], CL[:, COLS["rba"]:COLS["rba"] + 8], -1.0, None, ALU.mult, None, ['CL', 'CD'], ['CD'])
            ts('dve', CD[:, 24:32], CL[:, COLS["rbx"]:COLS["rbx"] + 8], -1.0, None, ALU.mult, None, ['CL', 'CD'], ['CD'])

            units = order_in if order_in is not None else []
            order_rec = []
            wst = {"issued": 0, "consumed": 0, "free": list(range(NS)), "slot_of": {}}

            def prefetch():
                if dry:
                    return
                while wst["issued"] < len(units) and wst["free"]:
                    i = wst["issued"]
                    k, u = units[i]
                    L = wshapes[k][1]
                    s = wst["free"].pop(0)
                    wst["slot_of"][i] = s
                    S.dma('sp', WR[:, s, 0:L], d_ws[k][u], 'wr%d' % s, reads=[("ws", k, u)], writes=[("WR", s)])
                    wst["issued"] += 1

            def consume(kind, u):
                i = wst["consumed"]
                order_rec.append((kind, u))
                wst["consumed"] += 1
                if dry:
                    return i % NS
                assert units[i] == (kind, u), (units[i], kind, u)
                assert wst["issued"] > i, "unit not prefetched (ring too small for this consumption pattern)"
                return wst["slot_of"][i]

            def release(slots):
                if dry:
                    return
                for sl in slots:
                    assert sl not in wst["free"]
                    wst["free"].append(sl)
                prefetch()

            prefetch()

            def sumsq(src, skey, src_all):
                pn, pk = ps_full()
                if src_all is not None:
                    allk = [k for c in range(8) for k in skey(c)]
                    sqk = [("HID", i) for i in range(8)]
                    act(HID[:, 0:8, :].rearrange("p a b -> p (a b)"), src_all, AF.Square, allk, sqk)
                    mm(pn[:], [(ONESB[:], HID[:, c, :]) for c in range(8)], sqk + ['ONESB'], pk)
                else:
                    for c in range(8):
                        b = c % 2
                        act(SQ[:, b, :], src(c), AF.Square, skey(c), [("SQ", b)])
                        S.op('pe', lambda E, b=b, c=c: E.matmul(pn[:], lhsT=ONESB[:], rhs=SQ[:, b, :], start=(c == 0),
                                                              stop=(c == 7)), [("SQ", b), 'ONESB'], pk)
                return pn, pk

            def rmsnorm(src, skey, gname, dst, dkey, nrm_scale=1.0 / D_MODEL, src_all=None):
                pn, pk = sumsq(src, skey, src_all)
                act(RB[:], pn[:], AF.Ln, pk, ['RB'], scale=nrm_scale, bias=EPS)
                act(RB[:], RB[:], AF.Exp, ['RB'], ['RB'], scale=-0.5)
                for c in range(8):
                    stt(dst(c), src(c), col(gname, c), RB[:], ALU.mult, ALU.mult, skey(c) + ['RB', 'CL'], dkey(c))

            Xall = lambda: XB[:, cur["p"]].rearrange("p a b -> p (a b)")
            Xc = lambda c: XB[:, cur["p"], c, :]
            Xk = lambda c: [("X", cur["p"], c)]
            XNc = lambda c: XN[:, c, :]
            XNk = lambda c: [("XN", c)]
            XN_all = [("XN", c) for c in range(8)]
            XNFc = lambda c: XNF[:, c, :]
            XNFk = lambda c: [("XNF", c)]
            XNF_all = [("XNF", c) for c in range(8)]

            def ffn_gen(gu, dd, par):
                Xc_ = lambda c: XB[:, par, c, :]
                Xk_ = lambda c: [("X", par, c)]
                for j in range(NFF):
                    u, jj = divmod(j, 2)
                    if jj == 0:
                        slot = consume(gu, u)
                    wv = WR[:, slot, :].rearrange("p (a b c d) -> p a b c d", a=2, b=2, c=8)
                    pg, pgk = ps_full()
                    pu, puk = ps_full()
                    if j == 0:
                        mm_seq(pg[:], [(wv[:, jj, 0, kc, :], XNF[:, kc, :]) for kc in range(8)],
                               [[("XNF", kc)] for kc in range(8)], [("WR", slot)], pgk)
                    else:
                        mm(pg[:], [(wv[:, jj, 0, kc, :], XNF[:, kc, :]) for kc in range(8)], XNF_all + [("WR", slot)], pgk)
                    mm(pu[:], [(wv[:, jj, 1, kc, :], XNF[:, kc, :]) for kc in range(8)], XNF_all + [("WR", slot)], puk)
                    if jj == 1:
                        release([slot])
                    a = j % 2
                    ga, gb = GT[:, a, :], GT[:, 2 + a, :]
                    sigmoid_chain(ga, pg[:], pgk, [("GT", a)])
                    tt('dve', gb, ga, pg[:], ALU.mult, [("GT", a)] + pgk, [("GT", 2 + a)])
                    tt('dve', HID[:, j, :], gb, pu[:], ALU.mult, [("GT", 2 + a)] + puk, [("HID", j)])
                    yield
                for oc in range(8):
                    slot = consume(dd, oc)
                    wv = WR[:, slot, 0:NFF * 128].rearrange("p (j m) -> p j m", j=NFF)
                    po, pok = ps_full()
                    mm(po[:], [(wv[:, j, :], HID[:, j, :]) for j in range(NFF)],
                       [("HID", j) for j in range(NFF)] + [("WR", slot)], pok)
                    release([slot])
                    stt(Xc_(oc), po[:], 0.5, Xc_(oc), ALU.mult, ALU.add, pok + Xk_(oc), Xk_(oc))
                    yield

            def ffn(gu, dd):
                for _ in ffn_gen(gu, dd, cur["p"]):
                    pass

            pumpst = {"it": None}

            def pump(n):
                it = pumpst["it"]
                if it is None:
                    return
                for _ in range(n):
                    if next(it, "end") == "end":
                        pumpst["it"] = None
                        return

            def win_chunk(m, slotmap):
                u, mmi = divmod(m, 4)
                slot = slotmap[u]
                wv = WR[:, slot, :].rearrange("p (a c d) -> p a c d", a=4, c=8)
                pp, ppk = ps_full()
                if m in (0, 16):
                    mm_seq(pp[:], [(wv[:, mmi, kc, :], XN[:, kc, :]) for kc in range(8)],
                           [[("XN", kc)] for kc in range(8)], [("WR", slot)], ppk)
                else:
                    mm(pp[:], [(wv[:, mmi, kc, :], XN[:, kc, :]) for kc in range(8)], XN_all + [("WR", slot)], ppk)
                return pp, ppk

            def pipelined(n, stages, width=2, npump=0):
                for i0 in range(0, n, width):
                    ctxs = [dict(c=i) for i in range(i0, min(n, i0 + width))]
                    for stg in stages:
                        for cx in ctxs:
                            stg(cx)
                    pump(npump)

            def conv_a(pp, ppk, kind, c, b):
                cp('act', XH[:, b, 3:T + 3], pp[:], ppk, [("XH", b)])
                cp('pool', XH[:, b, 0:3], HALO[:, kind, c, :], [("HALO", kind, c)], [("XH", b)])

            def conv_b(kind, c, wname, bname, dst, dkey, b):
                ts('dve', dst, XH[:, b, 0:T], col(wname, 0 * 8 + c), col(bname, c), ALU.mult, ALU.add,
                   [("XH", b), 'CL'], dkey)
                for tap in range(1, 4):
                    stt(dst, XH[:, b, tap:tap + T], col(wname, tap * 8 + c), dst, ALU.mult, ALU.add,
                        [("XH", b), 'CL'] + dkey, dkey)
                cp('pool', HALO[:, kind, c, :], XH[:, b, T:T + 3], [("XH", b)], [("HALO", kind, c)])

            def mlstm_front(slotmap, full=True):
                def f0(cx):
                    c = cx["c"]; b = c % 2
                    cx["cv"], cx["cvk"] = GT[:, 4 + b, :], [("GT", 4 + b)]
                    cx["sg"], cx["sgk"] = GT[:, 6 + b, :], [("GT", 6 + b)]
                    pp, ppk = win_chunk(c, slotmap)
                    conv_a(pp, ppk, 0, c, b)

                def f1(cx):
                    c = cx["c"]; b = c % 2
                    conv_b(0, c, "mcw", "mcb", cx["cv"], cx["cvk"], b)
                    cp('act', MIX[:, c, :], XH[:, b, 3:T + 3], [("XH", b)], [("MIX", c)])

                def f2(cx):
                    sigmoid_chain(cx["sg"], cx["cv"], cx["cvk"], cx["sgk"])

                def f3(cx):
                    c = cx["c"]
                    cv, cvk, sg, sgk = cx["cv"], cx["cvk"], cx["sg"], cx["sgk"]
                    tt('dve', MIX[:, 8 + c, :], cv, sg, ALU.mult, cvk + sgk, [("MIX", 8 + c)])
                    if full:
                        stt(B2[:, c, :], cv, col("mskip", c), sg, ALU.mult, ALU.mult, cvk + sgk + ['CL'], [("B2", c)])

                pipelined(8, [f0, f1, f2, f3], npump=0 if full else 2)
                release([slotmap[0], slotmap[1]])
                if full:
                    for w, src0 in ((0, 8), (1, 8), (2, 0)):
                        for c in range(8):
                            pp, ppk = ps_full()
                            mm(pp[:], [(WQKV[:, w, c, :], MIX[:, src0 + c, :])], [("MIX", src0 + c), 'WQKV'], ppk)
                            cp('act' if (c % 2) else 'dve', HID[:, w * 8 + c, :], pp[:], ppk, [("HID", w * 8 + c)])
                    return None
                qslot = lambda i: (B2B[:, i // 2, (i % 2) * T:(i % 2 + 1) * T], [("B2", i // 2)])
                pg, pgk = ps_full(hold=True)
                for rnd, ws in ((0, (0, 1)), (1, (2,))):
                    for wi, w in enumerate(ws):
                        src0 = 0 if w == 2 else 8
                        for c in range(8):
                            pp, ppk = ps_full()
                            mm(pp[:], [(WQKV[:, w, c, :], MIX[:, src0 + c, :])], [("MIX", src0 + c), 'WQKV'], ppk)
                            qa, qk = qslot(wi * 8 + c)
                            cp('act' if (c % 2) else 'dve', qa, pp[:], ppk, qk)
                    pump(2)
                    nsl = 8 * len(ws)
                    for s in range(NSUB):
                        ths = []
                        rk = []
                        for i in range(nsl):
                            qa, qk = qslot(i)
                            gi = i if rnd == 0 else 16 + i
                            first = (rnd == 0 and s == 0 and i == 0)
                            last = (rnd == 1 and s == NSUB - 1 and i == nsl - 1)
                            ths.append(lambda E, qa=qa, gi=gi, first=first, last=last, s=s: E.matmul(
                                pg[:, s * 8:(s + 1) * 8], lhsT=qa[:, s * 128:(s + 1) * 128], rhs=WGT[:, gi, :],
                                start=first, stop=last))
                            rk += qk
                        S.op('pe', ths, rk + ['WGT'], pgk)
                ps_release(pgk)
                return pg, pgk

            def mlstm_gates(pre=None):
                if pre is not None:
                    pg, pgk = pre
                else:
                    pg, pgk = ps_q()
                    for s in range(NSUB):
                        mm(pg[:, s * 8:(s + 1) * 8],
                           [(HID[:, i, s * 128:(s + 1) * 128], WGT[:, i, :]) for i in range(24)],
                           [("HID", i) for i in range(24)] + ['WGT'], pgk)
                Gf = G[:].rearrange("p a b -> p (a b)")
                tt('dve', Gf, pg[:, 0:32], BG[:].rearrange("p a b -> p (a b)"), ALU.add, pgk + ['BG'], ['G'])
                act(E1[:], G[:, :, 4:8], AF.Exp, ['G'], ['E1'], scale=-1.0)
                act(E1[:], E1[:], AF.Ln, ['E1'], ['E1'], bias=1.0)
                pb, pbk = ps_q()
                S.tag = S.tag + "F"
                for s in range(NSUB):
                    mm(pb[:, s * 4:(s + 1) * 4], [(UF[:], E1[:, s, :])], ['UF', 'E1'], pbk)
                for s in range(NSUB):
                    mm(pb[:, 16 + s * 4:16 + (s + 1) * 4], [(ONESF[:], E1[:, s, :])], ['ONESF', 'E1'], pbk)
                RRf = RR[:].rearrange("p a b -> p (a b)")
                CCf = CC[:].rearrange("p a b -> p (a b)")
                EGf = EG[:].rearrange("p a b -> p (a b)")
                act(RRf, pb[:, 0:16], AF.Exp, pbk, ['RR'], scale=-1.0)
                tt('dve', CC[:], G[:, :, 0:4], pb[:, 0:16].rearrange("p (a b) -> p a b", a=NSUB), ALU.add,
                   ['G'] + pbk, ['CC'])
                act(CCf, CCf, AF.Exp, ['CC'], ['CC'])
                act(EGf, pb[:, 16:32], AF.Exp, pbk, ['EG'], scale=-1.0)

            def mlstm_tok(s):
                kb = s % 2
                cols_s = slice(s * 128, (s + 1) * 128)
                pk0, pk0k = ps_full()
                pk1, pk1k = ps_full()
                pv0, pv0k = ps_full()
                pv1, pv1k = ps_full()
                for c in range(8):
                    pk_, pkk = (pk0, pk0k) if c < 4 else (pk1, pk1k)
                    mm(pk_[:, (c % 4) * 128:(c % 4 + 1) * 128], [(MIX[:, 8 + c, cols_s], WQKV[:, 1, c, :])],
                       [("MIX", 8 + c), 'WQKV'], pkk)
                for c in range(8):
                    pv_, pvk = (pv0, pv0k) if c < 4 else (pv1, pv1k)
                    mm(pv_[:, (c % 4) * 128:(c % 4 + 1) * 128], [(MIX[:, c, cols_s], WQKV[:, 2, c, :])],
                       [("MIX", c), 'WQKV'], pvk)
                kk = [("KTOK", kb)]
                act(KTOK[:, kb, 0:512], pk0[:], AF.Copy, pk0k, kk, scale=1.0 / 16.0)
                ts('dve', KTOK[:, kb, 512:1024], pk1[:], 1.0 / 16.0, None, ALU.mult, None, pk1k, kk)
                vk = ['VTOK%d' % kb]
                for h in range(4):
                    pv_, pvk = (pv0, pv0k) if h < 2 else (pv1, pv1k)
                    act(VTOK[:, kb, h, 0:256], pv_[:, (h % 2) * 256:(h % 2 + 1) * 256], AF.Copy, pvk + ['CC'], vk,
                        scale=CC[:, s, h:h + 1])
                cp('dve', VTOK[:, kb, :, 256], CC[:, s, :], ['CC'], vk)

            def mlstm_state(s, shadow=True):
                kb = s % 2
                for h in range(4):
                    for e in range(2):
                        pu, puk = ps_full()
                        mm(pu[:, 0:257], [(KTOK[:, kb, (2 * h + e) * 128:(2 * h + e + 1) * 128], VTOK[:, kb, h, 0:257])],
                           [("KTOK", kb), 'VTOK%d' % kb], puk)
                        sk = [("ST", h, e)]
                        ts('dve', ST[:, h, e, 0:257], ST[:, h, e, 0:257], EG[:, s, h:h + 1], None, ALU.mult, None,
                           sk + ['EG'], sk)
                        stt(ST[:, h, e, 0:257], pu[:, 0:257], EG[:, s, h:h + 1], ST[:, h, e, 0:257], ALU.mult, ALU.add,
                            puk + sk + ['EG'], sk)
                        if shadow:
                            cp('act', STB[:, h, e, 0:257], ST[:, h, e, 0:257], sk, [("STB", h, e)])

            def mlstm_core(s):
                kb = s % 2
                cols_s = slice(s * 128, (s + 1) * 128)
                pS, pSk = ps_full()
                for h in range(4):
                    mm(pS[:, h * 128:(h + 1) * 128],
                       [(HID[:, 8 + 2 * h + e, cols_s], HID[:, 2 * h + e, cols_s]) for e in range(2)],
                       [("HID", 8 + 2 * h + e) for e in range(2)] + [("HID", 2 * h + e) for e in range(2)], pSk)
                for h in range(4):
                    stt(PT[:, h, :], pS[:, h * 128:(h + 1) * 128], 1.0 / 16.0, UF[:], ALU.mult, ALU.mult,
                        pSk + ['UF'], [("PT", h)])
                for h in range(4):
                    pn, pnk = ps_full()
                    mm(pn[:, 0:257],
                       [(PT[:, h, :], VTOK[:, kb, h, 0:257])] +
                       [(HID[:, 2 * h + e, cols_s], STB[:, h, e, 0:257]) for e in range(2)],
                       [("PT", h), 'VTOK%d' % kb] + [("HID", 2 * h + e) for e in range(2)] +
                       [("STB", h, e) for e in range(2)], pnk)
                    cp('act', NUMS[:, h, 0:257], pn[:, 0:257], pnk, [("NUMS", h)])

            def mlstm_post(s):
                cols_s = slice(s * 128, (s + 1) * 128)
                nk = [("NUMS", h) for h in range(4)]
                tt('dve', SM[:, 0, :], NUMS[:, :, 256], RR[:, s, :], ALU.mult, nk + ['RR'], ['SM0'])
                act(SM[:, 0, :], SM[:, 0, :], AF.Abs, ['SM0'], ['SM0'])
                ts('dve', SM[:, 0, :], SM[:, 0, :], 1.0, None, ALU.max, None, ['SM0'], ['SM0'])
                S.op('dve', lambda E: E.reciprocal(out=SM[:, 0, :], in_=SM[:, 0, :]), ['SM0'], ['SM0'])
                tt('dve', SM[:, 1, :], SM[:, 0, :], RR[:, s, :], ALU.mult, ['SM0', 'RR'], ['SM1'])
                for h in range(4):
                    hk = [("GT", 8 + h // 2)]
                    act(HS[:, h * 256:(h + 1) * 256], NUMS[:, h, 0:256], AF.Copy, [("NUMS", h), 'SM1'], hk,
                        scale=SM[:, 1, h:h + 1])
                    S.op('dve', lambda E, h=h: E.bn_stats(out=BNS[:, h, :], in_=HS[:, h * 256:(h + 1) * 256]), hk, [("BNS", h)])
                    S.op('dve', lambda E, h=h: E.bn_aggr(out=MV[:, h, :], in_=BNS[:, h, :]), [("BNS", h)], [("MV", h)])
                mvk = [("MV", h) for h in range(4)]
                act(SM[:, 2, :], MV[:, :, 1], AF.Ln, mvk, ['SM2'], bias=EPS)
                act(SM[:, 2, :], SM[:, 2, :], AF.Exp, ['SM2'], ['SM2'], scale=-0.5)
                for h in range(4):
                    ts('dve', HN[:, h * 256:(h + 1) * 256], HS[:, h * 256:(h + 1) * 256], MV[:, h, 0:1], SM[:, 2, h:h + 1],
                       ALU.subtract, ALU.mult, [("GT", 8 + h // 2), ("MV", h), 'SM2'], [("GT", 6 + h // 2)])

            def mlstm_postT(s):
                cols_s = slice(s * 128, (s + 1) * 128)
                for half in range(2):
                    ptb, ptk = ps_full()
                    for cc in range(4):
                        c = half * 4 + cc
                        S.op('pe', lambda E, c=c, cc=cc, ptb=ptb: E.transpose(ptb[:, cc * 128:(cc + 1) * 128],
                                                                             HN[:, c * 128:(c + 1) * 128], IDENT[:]),
                             [("GT", 6 + c // 4), 'IDENT'], ptk)
                    for cc in range(4):
                        c = half * 4 + cc
                        stt(B2[:, c, cols_s], ptb[:, cc * 128:(cc + 1) * 128], col("mln", c), B2[:, c, cols_s],
                            ALU.mult, ALU.add, ptk + [("B2", c), 'CL'], [("B2", c)])

            def mlstm_finish(slotmap, before_b2=None):
                def zpart(c):
                    a = c % 2
                    pp, ppk = win_chunk(8 + c, slotmap)
                    sg, sgk = GT[:, 4 + a, :], [("GT", 4 + a)]
                    sigmoid_chain(sg, pp[:], ppk, sgk)
                    tt('dve', sg, sg, pp[:], ALU.mult, sgk + ppk, sgk)
                    return sg, sgk

                def bpart(c, sg, sgk):
                    tt('dve', B2[:, c, :], B2[:, c, :], sg, ALU.mult, [("B2", c)] + sgk, [("B2", c)])

                z0 = zpart(0)
                z1 = zpart(1)
                if before_b2 is not None:
                    before_b2()
                bpart(0, *z0)
                bpart(1, *z1)
                for c in range(2, 8):
                    bpart(c, *zpart(c))
                release([slotmap[2], slotmap[3]])
                dump("out_m", B2[:], [P, 8, T], [("B2", c) for c in range(8)])
                rmsnorm(lambda c: B2[:, c, :], lambda c: [("B2", c)], "gom",
                        lambda c: MIX[:, c, :], lambda c: [("MIX", c)], src_all=B2[:].rearrange("p a b -> p (a b)"))

            B3c = lambda c: HIDF[:, 8 + 2 * c:8 + 2 * c + 2, :].rearrange("p a b -> p (a b)")
            B3k = lambda c: [("HID", 8 + 2 * c), ("HID", 8 + 2 * c + 1)]

            def rglru(slotmap, full):
                def c0(cx):
                    c = cx["c"]
                    pp, ppk = win_chunk(16 + c, slotmap)
                    conv_a(pp, ppk, 1, c, c % 2)

                if full:
                    xcrb = lambda c: (HID[:, c, :], ("HID", c))
                    tslots = [[(GT[:, 5 * pp_ + i, :], [("GT", 5 * pp_ + i)]) for i in range(5)] for pp_ in range(2)]
                else:
                    xcrb = lambda c: (MIX[:, c, :], ("MIX", c))
                    nk_ = [("NUMS", h) for h in range(4)]
                    tslots = [[(GT[:, 4 + i, :], [("GT", 4 + i)]) for i in range(5)],
                              [(GT[:, 9, :], [("GT", 9)]), (NUMSF[:, 0:T], nk_), (NUMSF[:, T:2 * T], nk_),
                               (KTOKF[:, 0, :], [("KTOK", 0)]), (KTOKF[:, 1, :], [("KTOK", 1)])]]

                def c1(cx):
                    c = cx["c"]
                    conv_b(1, c, "rcw", "rcb", B2[:, c, :], [("B2", c)], c % 2)
                    cp('act', xcrb(c)[0], B2[:, c, :], [("B2", c)], [xcrb(c)[1]])

                pipelined(8, [c0, c1], npump=0 if full else 1)
                release([slotmap[4], slotmap[5]])
                if full:
                    S.tag = S.tag.replace("rglru", "woutm")
                    wout_half("woutm", 0)
                    S.tag = S.tag.replace("woutm", "rglru")
                    slotmap[6] = consume("win", 6)
                    slotmap[7] = consume("win", 7)
                def g0(cx):
                    c = cx["c"]; n = c // 2; oc = (c % 2) * 128
                    sl = tslots[c % 2]
                    cx["g"] = lambda i, sl=sl: sl[i][0]
                    cx["gk"] = lambda i, sl=sl: list(sl[i][1])
                    cx["pa"], cx["pak"] = ps_full()
                    cx["px"], cx["pxk"] = ps_full()
                    rd = [xcrb(2 * n)[1], xcrb(2 * n + 1)[1], 'WRG']
                    mm(cx["pa"][:], [(WRG[:, 0, n, kc, oc:oc + 128], xcrb(2 * n + kc)[0]) for kc in range(2)], rd, cx["pak"])
                    mm(cx["px"][:], [(WRG[:, 1, n, kc, oc:oc + 128], xcrb(2 * n + kc)[0]) for kc in range(2)], rd, cx["pxk"])
                    if full:
                        cx["py"], cx["pyk"] = win_chunk(24 + c, slotmap)

                sig = sigmoid_chain

                def gA(cx):
                    c = cx["c"]
                    if full:
                        act(B3c(c), cx["py"][:], AF.Square, cx["pyk"], B3k(c), scale=0.044715 ** 0.5)
                        stt(B3c(c), B3c(c), 1.0, cx["py"][:], ALU.add, ALU.mult, B3k(c) + cx["pyk"], B3k(c))

                def gB(cx):
                    c = cx["c"]
                    if full:
                        sigmoid_chain(B3c(c), B3c(c), B3k(c), B3k(c), scale=-GELU_C)

                def gC(cx):
                    c = cx["c"]
                    if full:
                        tt('dve', B3c(c), B3c(c), cx["py"][:], ALU.mult, B3k(c) + cx["pyk"], B3k(c))

                def g1(cx):
                    c, g, gk = cx["c"], cx["g"], cx["gk"]
                    sig(g(0), cx["pa"][:], cx["pak"] + ['CD'], gk(0), scale=-1.0, bias=CD[:, 16 + c:17 + c])

                def g2(cx):
                    c, g, gk = cx["c"], cx["g"], cx["gk"]
                    sig(g(1), cx["px"][:], cx["pxk"] + ['CD'], gk(1), scale=-1.0, bias=CD[:, 24 + c:25 + c])

                def g3(cx):
                    c, g, gk = cx["c"], cx["g"], cx["gk"]
                    act(g(2), g(0), AF.Exp, gk(0) + ['CD'], gk(2), scale=CD[:, c:c + 1])
                    act(g(3), g(0), AF.Exp, gk(0) + ['CD'], gk(3), scale=CD[:, 8 + c:9 + c])
                    act(g(3), g(3), AF.Ln, gk(3), gk(3), scale=-1.0, bias=1.0)
                    act(g(3), g(3), AF.Exp, gk(3), gk(3), scale=0.5)
                    tt('pool', g(1), g(1), B2[:, c, :], ALU.mult, gk(1) + [("B2", c)], gk(1))

                def g5(cx):
                    c, g, gk = cx["c"], cx["g"], cx["gk"]
                    tt('dve', g(1), g(1), g(3), ALU.mult, gk(1) + gk(3), gk(1))
                    S.op('dve', lambda E, c=c, o=g(4), d0=g(2), d1=g(1): E.tensor_tensor_scan(
                        out=o, data0=d0, data1=d1, initial=HST[:, c:c + 1], op0=ALU.mult, op1=ALU.add),
                         gk(2) + gk(1) + [("HST", c)], gk(4))
                    cp('pool', HST[:, c:c + 1], g(4)[:, T - 1:T], gk(4), [("HST", c)])

                def g7(cx):
                    c, g, gk = cx["c"], cx["g"], cx["gk"]
                    if full:
                        tt('pool', B3c(c), B3c(c), g(4), ALU.mult, gk(4) + B3k(c), B3k(c))

                pipelined(8, [g0, gA, gB, gC, g1, g2, g3, g5, g7], npump=0 if full else 2)
                if full:
                    release([slotmap[6], slotmap[7]])
                if full:
                    dump("out_r", HIDF[:, 8:24, :], [P, 16, 256], [("HID", i) for i in range(8, 24)])
                    dump("xcr", B2[:], [P, 8, T], [("B2", c) for c in range(8)])
                    rmsnorm(B3c, B3k, "gor", lambda c: MIX[:, 8 + c, :], lambda c: [("MIX", 8 + c)],
                            src_all=HIDF[:, 8:24, :].rearrange("p a b -> p (a b)"))

            def wout_half(kind, kbase):
                slots = {}
                for oc in range(8):
                    u, oo = divmod(oc, 4)
                    if oo == 0:
                        slots[u] = consume(kind, u)
                    wv = WR[:, slots[u], :].rearrange("p (a c d) -> p a c d", a=4, c=8)
                    po, pok = ps_full()
                    if oc == 0:
                        mm_seq(po[:], [(wv[:, oo, kc, :], MIX[:, kbase + kc, :]) for kc in range(8)],
                               [[("MIX", kbase + kc)] for kc in range(8)], [("WR", slots[u])], pok)
                    else:
                        mm(po[:], [(wv[:, oo, kc, :], MIX[:, kbase + kc, :]) for kc in range(8)],
                           [("MIX", kbase + kc) for kc in range(8)] + [("WR", slots[u])], pok)
                    if oo == 3:
                        release([slots[u]])
                    tt('dve', Xc(oc), po[:], Xc(oc), ALU.add, pok + Xk(oc), Xk(oc))

            def load_x(src, t, par):
                for c in range(8):
                    S.dma('sp', XB[:, par, c, :], src[c * P:(c + 1) * P, t * T:(t + 1) * T], 'ldx%d_%d' % (par, c),
                          writes=[("X", par, c)])

            def apply_flag():
                fl = col("flag")
                for h in range(4):
                    for e in range(2):
                        sk = [("ST", h, e)]
                        ts('dve', ST[:, h, e, 0:257], ST[:, h, e, 0:257], fl, None, ALU.mult, None, sk + ['CL'], sk)
                        cp('pool', STB[:, h, e, 0:257], ST[:, h, e, 0:257], sk, [("STB", h, e)])
                ts('dve', HST[:], HST[:], fl, None, ALU.mult, None, [("HST", c) for c in range(8)] + ['CL'],
                   [("HST", c) for c in range(8)])
                hf = HALO[:].rearrange("p a b c -> p (a b c)")
                hk = [("HALO", k, c) for k in range(2) for c in range(8)]
                ts('dve', hf, hf, fl, None, ALU.mult, None, hk + ['CL'], hk)

            def tile_pass(src, t, full, tile_idx, skip_ffn1=False):
                tg = lambda name: setattr(S, 'tag', "%s%d:%s" % ("M" if full else "P", t, name))
                tg("norm1")
                cur["p"] = tile_idx % 2
                ensure_load(tile_idx)
                ensure_load(tile_idx + 1)
                if not skip_ffn1:
                    rmsnorm(Xc, Xk, "g1", XNFc, XNFk, src_all=Xall())
                    tg("ffn1")
                    ffn("gu1", "d1")
                if not full and late:
                    n_now = 4 if tile_idx + 1 < n_pre else len(late)
                    emit_casts(late[:n_now], reads=Xk(7))
                    del late[:n_now]
                tg("normmix")
                dump("x1", XB[:, cur["p"]], [P, 8, T], [("X", cur["p"], c) for c in range(8)])
                rmsnorm(Xc, Xk, "gmix", XNc, XNk, src_all=Xall())
                slotmap = {}
                if full:
                    slotmap[0] = consume("win", 0)
                    slotmap[1] = consume("win", 1)
                    tg("mfront")
                    mlstm_front(slotmap)
                    dump("xs", B2[:], [P, 8, T], [("B2", c) for c in range(8)])
                    tg("mgates")
                    mlstm_gates()
                    dump("G", G[:], [P, NSUB, 8], ['G'])
                    dump("RR", RR[:], [P, NSUB, 4], ['RR'])
                    dump("CC", CC[:], [P, NSUB, 4], ['CC'])
                    dump("EG", EG[:], [P, NSUB, 4], ['EG'])
                    tg("mtok0")
                    mlstm_tok(0)
                    for s in range(NSUB):
                        tg("mout%d" % s)
                        mlstm_core(s)
                        tg("mstate%d" % s)
                        mlstm_state(s)
                        if s > 0:
                            tg("mpost%d" % (s - 1))
                            mlstm_postT(s - 1)
                        if s + 1 < NSUB:
                            tg("mtok%d" % (s + 1))
                            mlstm_tok(s + 1)
                        tg("mpost%d" % s)
                        mlstm_post(s)
                    dump("hnT", B2[:], [P, 8, T], [("B2", c) for c in range(8)])
                    slotmap[2] = consume("win", 2)
                    slotmap[3] = consume("win", 3)
                    tg("mfinish")
                    mlstm_finish(slotmap, before_b2=lambda: mlstm_postT(NSUB - 1))
                    tg("rglru")
                    slotmap[4] = consume("win", 4)
                    slotmap[5] = consume("win", 5)
                    rglru(slotmap, True)
                    tg("woutr")
                    wout_half("woutr", 8)
                    dump("x2", XB[:, cur["p"]], [P, 8, T], [("X", cur["p"], c) for c in range(8)])
                    tg("norm2")
                    rmsnorm(Xc, Xk, "g2", XNFc, XNFk, src_all=Xall())
                    tg("ffn2")
                    ffn("gu2", "d2")
                    tg("normfin")
                    dump("x3", XB[:, cur["p"]], [P, 8, T], [("X", cur["p"], c) for c in range(8)])
                    ob = lambda c: GT[:, 8 + c % 2, :]
                    obk = lambda c: [("GT", 8 + c % 2)]
                    pn, pk = sumsq(Xc, Xk, Xall())
                    act(RB[:], pn[:], AF.Ln, pk, ['RB'], scale=1.0 / D_MODEL, bias=EPS)
                    act(RB[:], RB[:], AF.Exp, ['RB'], ['RB'], scale=-0.5)
                    for c in range(8):
                        stt(ob(c), Xc(c), col("gfin", c), RB[:], ALU.mult, ALU.mult, Xk(c) + ['RB', 'CL'], obk(c))
                        S.dma('sp', d_out[c * P:(c + 1) * P, t * T:(t + 1) * T], ob(c), 'sto%d' % (c % 2), reads=obk(c))

            def prefix_mixer(t):
                tg = lambda name: setattr(S, 'tag', "P%d:%s" % (t, name))
                slotmap = {}
                slotmap[0] = consume("win", 0)
                slotmap[1] = consume("win", 1)
                tg("mfront")
                pre = mlstm_front(slotmap, False)
                tg("mgates")
                mlstm_gates(pre)
                pump(1)
                tg("mtok0")
                mlstm_tok(0)
                pump(1)
                tg("mtok1")
                mlstm_tok(1)
                pump(1)
                for s in range(NSUB):
                    tg("mstate%d" % s)
                    mlstm_state(s, shadow=False)
                    if s + 2 < NSUB:
                        tg("mtok%d" % (s + 2))
                        mlstm_tok(s + 2)
                    pump(1)
                slotmap[4] = consume("win", 4)
                slotmap[5] = consume("win", 5)
                tg("rglru")
                rglru(slotmap, False)

            passes = [(d_xpT, t, False) for t in range(n_pre)] + [(d_xT, t, True) for t in range(n_main)]
            loaded = set()

            def ensure_load(i):
                if i < len(passes) and i not in loaded:
                    load_x(passes[i][0], passes[i][1], i % 2)
                    loaded.add(i)

            ensure_load(0)
            if n_pre > 0:
                ensure_load(1)
                cur["p"] = 0
                S.tag = "P0:norm1"
                rmsnorm(Xc, Xk, "g1", XNFc, XNFk, src_all=Xall())
                S.tag = "P0:ffn1"
                ffn("gu1", "d1")
                for t in range(n_pre):
                    par = t % 2
                    cur["p"] = par
                    if late:
                        n_now = 4 if t + 1 < n_pre else len(late)
                        emit_casts(late[:n_now], reads=Xk(7))
                        del late[:n_now]
                    S.tag = "P%d:normmix" % t
                    rmsnorm(Xc, Xk, "gmix", XNc, XNk, src_all=Xall())
                    ensure_load(t + 2)
                    if t + 1 < len(passes):
                        ensure_load(t + 1)
                        cur["p"] = 1 - par
                        S.tag = "P%d:norm1" % (t + 1)
                        rmsnorm(Xc, Xk, "g1", XNFc, XNFk, src_all=Xall())
                        cur["p"] = par
                        pumpst["it"] = ffn_gen("gu1", "d1", 1 - par)
                    prefix_mixer(t)
                    pump(10 ** 6)
                if late:
                    emit_casts(late)
                    del late[:]
                apply_flag()
            for i in range(n_pre, len(passes)):
                tile_pass(passes[i][0], passes[i][1], True, i, skip_ffn1=(i == n_pre and n_pre > 0))
            for b in range(2):
                S._wait('sp', ('sto%d' % b, S.cnt['sto%d' % b]))
            for name in dumps:
                S._wait('sp', ('dbg_' + name, S.cnt['dbg_' + name]))
            return S, order_rec

        _, order = body(None)
        S, _ = body(order)
        S.emit()
    nc._pe_tags = S.pe_tags
    return nc


def _cols_of(v):
    return np.ascontiguousarray(v.reshape(-1, P).T)


def _prep_shared(inp):
    f = lambda a: np.asarray(a, dtype=np.float32)
    sh = {}
    cols = np.zeros((P, NCOL), np.float32)

    def put(name, v):
        c = _cols_of(f(v).reshape(-1))
        cols[:, COLS[name]:COLS[name] + c.shape[1]] = c
    put("g1", inp["norm_ffn1"][0]); put("gmix", inp["norm_mix"][0]); put("g2", inp["norm_ffn2"][0])
    put("gfin", inp["norm_final"])
    for tap in range(4):
        cols[:, COLS["mcw"] + tap * 8:COLS["mcw"] + tap * 8 + 8] = _cols_of(f(inp["m_conv_w"][0][tap]))
        cols[:, COLS["rcw"] + tap * 8:COLS["rcw"] + tap * 8 + 8] = _cols_of(f(inp["r_conv_w"][0][tap]))
    put("mcb", inp["m_conv_b"][0]); put("rcb", inp["r_conv_b"][0]); put("rba", inp["r_b_a"][0])
    put("rbx", inp["r_b_x"][0]); put("rlam", inp["r_lam"][0]); put("mln", inp["m_ln_w"][0])
    put("mskip", inp["m_skip"][0]); put("gom", inp["out_norm_m"][0]); put("gor", inp["out_norm_r"][0])
    sh["cols"] = cols
    sh["bgates"] = np.ascontiguousarray(np.broadcast_to(np.tile(f(inp["m_b_gates"][0]), NSUB)[None, :], (P, NSUB * 8)))
    wqkv = np.zeros((P, 3, 8, 128), np.float32)
    for w, name in enumerate(["m_wq", "m_wk", "m_wv"]):
        blk = f(inp[name][0])
        for c in range(8):
            for j in range(32):
                wqkv[4 * j:4 * j + 4, w, c, 4 * j:4 * j + 4] = blk[c * 32 + j]
    sh["wqkv"] = wqkv.reshape(P, -1)
    wrg = np.zeros((P, 2, 4, 2, 256), np.float32)
    for w, name in enumerate(["r_w_a", "r_w_x"]):
        a = f(inp[name][0])
        wrg[:, w] = a.reshape(4, 2, P, 256).transpose(2, 0, 1, 3)
    sh["wrg"] = wrg.reshape(P, -1)
    sh["wgt"] = np.ascontiguousarray(f(inp["m_w_gates"][0]).reshape(24, P, 8).transpose(1, 0, 2)).reshape(P, -1)
    ffn_w = {"1": (inp["ffn1_wg"], inp["ffn1_wu"], inp["ffn1_wd"]), "2": (inp["ffn2_wg"], inp["ffn2_wu"], inp["ffn2_wd"])}
    for i in ("1", "2"):
        wg = f(ffn_w[i][0][0]).reshape(8, P, 11, 2, 128)
        wu = f(ffn_w[i][1][0]).reshape(8, P, 11, 2, 128)
        gu = np.stack([wg, wu], axis=0)
        sh["w_gu" + i] = np.ascontiguousarray(gu.transpose(3, 2, 4, 0, 1, 5)).reshape(11, P, SLOT)
        wd = f(ffn_w[i][2][0]).reshape(NFF, P, 8, 128)
        sh["w_d" + i] = np.ascontiguousarray(wd.transpose(2, 1, 0, 3)).reshape(8, P, NFF * 128)
    win = f(inp["w_in"][0]).reshape(8, P, 8, 4, 128)
    sh["w_win"] = np.ascontiguousarray(win.transpose(2, 1, 3, 0, 4)).reshape(8, P, SLOT)
    wout = f(inp["w_out"][0]).reshape(2, 8, P, 2, 4, 128)
    sh["w_woutm"] = np.ascontiguousarray(wout[0].transpose(2, 1, 3, 0, 4)).reshape(2, P, SLOT)
    sh["w_woutr"] = np.ascontiguousarray(wout[1].transpose(2, 1, 3, 0, 4)).reshape(2, P, SLOT)
    return sh


_CACHE = {}


def kernel(**inputs):
    x = np.asarray(inputs["x"], dtype=np.float32)
    B, SEQ, D = x.shape
    half = SEQ // 2
    n_tiles = half // T
    sh = _prep_shared(inputs)
    key = (n_tiles, half)
    if key not in _CACHE:
        _CACHE[key] = build_program(n_tiles, n_tiles, half, half)
    nc = _CACHE[key]
    in_maps = []
    for core in range(8):
        b, hh = divmod(core, 2)
        m = dict(sh)
        m["xT"] = np.ascontiguousarray(x[b, hh * half:(hh + 1) * half, :].T)
        m["xpT"] = np.ascontiguousarray(x[b, 0:half, :].T)
        cols = sh["cols"].copy()
        cols[:, COLS["flag"]] = float(hh)
        m["cols"] = cols
        for k in ("w_gu1", "w_d1", "w_win", "w_woutm", "w_woutr", "w_gu2", "w_d2"):
            a = sh[k]
            pad = np.full((1, a.shape[2]), float(core), np.float32)
            m[k] = np.concatenate([a.reshape(-1, a.shape[2]), pad], axis=0)
        in_maps.append(m)
    res = run_bass_kernel_spmd(nc, in_maps, core_ids=list(range(8)))
    out = np.empty((B, SEQ, D), np.float32)
    for core in range(8):
        b, hh = divmod(core, 2)
        out[b, hh * half:(hh + 1) * half, :] = res.results[core]["outT"].T
    return out
```

```python
import numpy as np
from contextlib import ExitStack
import concourse.bass as bass
import concourse.mybir as mybir
from concourse.bass_utils import run_bass_kernel_spmd

F32 = mybir.dt.float32
BF16 = mybir.dt.bfloat16
AF = mybir.ActivationFunctionType
ALU = mybir.AluOpType

P = 128
T = 512
NSUB = 4
NFF = 22
D_MODEL = 1024
D_FF = 2816
EPS = 1e-6
NS = 4
SLOT = 4096
GELU_C = 1.5957691216057308

COLS = {}
_o = 0
for _n, _w in [("g1", 8), ("gmix", 8), ("g2", 8), ("gfin", 8), ("mcw", 32), ("mcb", 8), ("rcw", 32), ("rcb", 8),
               ("rba", 8), ("rbx", 8), ("rlam", 8), ("mln", 8), ("mskip", 8), ("gom", 8), ("gor", 8), ("flag", 1)]:
    COLS[_n] = _o
    _o += _w
NCOL = _o


class Sched:
    ENGS = ('pe', 'act', 'dve', 'pool', 'sp')

    def __init__(self, nc, stack, dry=False):
        self.nc = nc
        self.stack = stack
        self.dry = dry
        self.lists = {e: [] for e in self.ENGS}
        self.sems = {}
        self.cnt = {}
        self.known = {e: {} for e in self.ENGS}
        self.lastw = {}
        self.readers = {}
        self.nins = {e: 0 for e in self.ENGS}
        self.evs = {}
        self.tag = "setup"
        self.pe_tags = []
        for e in self.ENGS:
            self._mk(e)

    def fix_total(self, semname):
        for ev in self.evs.get(semname, []):
            ev[1] = self.cnt[semname]

    def _mk(self, name):
        self.sems[name] = None if self.dry else self.stack.enter_context(self.nc.semaphore(name))
        self.cnt[name] = 0

    def _wait(self, eng, ev):
        if ev is None:
            return
        s, v = ev
        if eng == 'pe' and s == 'pe':
            return
        if self.known[eng].get(s, 0) >= v:
            return
        self.known[eng][s] = v
        sem = self.sems[s]
        self.lists[eng].append(lambda E, sem=sem, v=v: E.wait_ge(sem, v))

    def _deps(self, eng, reads, writes):
        for k in reads:
            self._wait(eng, self.lastw.get(k))
        for k in writes:
            self._wait(eng, self.lastw.get(k))
            for ev in self.readers.get(k, ()):
                self._wait(eng, ev)

    def _commit(self, ev, reads, writes):
        for k in reads:
            self.readers.setdefault(k, []).append(ev)
        for k in writes:
            self.lastw[k] = ev
            self.readers[k] = []

    def op(self, eng, thunks, reads=(), writes=()):
        if callable(thunks):
            thunks = [thunks]
        self._deps(eng, reads, writes)
        self.cnt[eng] += 1
        ev = (eng, self.cnt[eng])
        sem = self.sems[eng]
        n = len(thunks)
        self.nins[eng] += n
        if eng == 'pe':
            self.pe_tags += [self.tag] * n
        for i, th in enumerate(thunks):
            if i == n - 1:
                self.lists[eng].append(lambda E, th=th, sem=sem: th(E).then_inc(sem, 1))
            else:
                self.lists[eng].append(th)
        self._commit(ev, reads, writes)
        return ev

    def dma(self, q, out, in_, semname, reads=(), writes=()):
        if semname not in self.sems:
            self._mk(semname)
        self._deps(q, reads, writes)
        self.cnt[semname] += 16
        ev = [semname, self.cnt[semname]]
        self.evs.setdefault(semname, []).append(ev)
        sem = self.sems[semname]
        self.lists[q].append(lambda E, out=out, in_=in_, sem=sem: E.dma_start(out=out, in_=in_).then_inc(sem, 16))
        self._commit(ev, reads, writes)
        return ev

    def emit(self):
        nc = self.nc
        L = self.lists
        with nc.Block() as block:
            @block.tensor
            def _(E):
                for th in L['pe']:
                    th(E)

            @block.scalar
            def _(E):
                for th in L['act']:
                    th(E)

            @block.vector
            def _(E):
                for th in L['dve']:
                    th(E)

            @block.gpsimd
            def _(E):
                for th in L['pool']:
                    th(E)

            @block.sync
            def _(E):
                for th in L['sp']:
                    th(E)


def build_program(n_pre, n_main, ntok_pre, ntok_main, dbg=False):
    nc = bass.Bass("TRN2", target_bir_lowering=False)
    dr = lambda name, shape, dt=F32, kind="ExternalInput": nc.dram_tensor(name, list(shape), dt, kind=kind).ap()
    d_xT = dr("xT", [D_MODEL, ntok_main])
    d_xpT = dr("xpT", [D_MODEL, max(ntok_pre, T)])
    d_out = dr("outT", [D_MODEL, ntok_main], kind="ExternalOutput")
    d_cols = dr("cols", [P, NCOL])
    d_bg = dr("bgates", [P, NSUB * 8])
    d_wqkv = dr("wqkv", [P, 3 * 8 * 128])
    d_wrg = dr("wrg", [P, 2 * 4 * 2 * 256])
    d_wgt = dr("wgt", [P, 24 * 8])
    wshapes = {"gu1": (11, SLOT), "d1": (8, NFF * 128), "win": (8, SLOT), "woutm": (2, SLOT), "woutr": (2, SLOT),
               "gu2": (11, SLOT), "d2": (8, NFF * 128)}
    d_w2 = {k: dr("w_" + k, [n * P + 1, L]) for k, (n, L) in wshapes.items()}
    d_w = {k: [d_w2[k][u * P:(u + 1) * P, :] for u in range(n)] for k, (n, L) in wshapes.items()}
    d_ws = {k: dr("ws_" + k, [n, P, L], BF16, kind="Internal") for k, (n, L) in wshapes.items()}
    dbg_out = {}

    with ExitStack() as st:
        sb = lambda name, shape, dt: st.enter_context(nc.sbuf_tensor(name, list(shape), dt))
        XB = sb("X", [P, 2, 8, T], F32)
        cur = {"p": 0}
        XN = sb("XN", [P, 8, T], BF16)
        SQ = sb("SQ", [P, 2, T], BF16)
        RB = sb("RB", [P, T], F32)
        HID = sb("HID", [P, 24, T], BF16)
        HIDF = HID.bitcast(F32)
        WR = sb("WR", [P, NS, SLOT], BF16)
        GT = sb("GT", [P, 10, T], F32)
        XH = sb("XH", [P, 2, T + 3], F32)
        HS = GT[:, 8:10, :].rearrange("p a b -> p (a b)")
        HN = GT[:, 6:8, :].rearrange("p a b -> p (a b)")
        XNF = sb("XNF", [P, 8, T], BF16)
        B2B = None
        HALO = sb("HALO", [P, 2, 8, 3], F32)
        B2 = sb("B2", [P, 8, T], F32)
        MIX = sb("MIX", [P, 16, T], BF16)
        B2B = B2.bitcast(BF16)
        ST = sb("ST", [P, 4, 2, 260], F32)
        STB = sb("STB", [P, 4, 2, 260], BF16)
        KTOK = sb("KTOK", [P, 2, 1024], BF16)
        VTOK = sb("VTOK", [P, 2, 4, 260], BF16)
        NUMS = sb("NUMS", [P, 4, 260], F32)
        NUMSF = NUMS[:].rearrange("p a b -> p (a b)")
        KTOKF = KTOK.bitcast(F32)
        PT = sb("PT", [P, 4, 128], BF16)
        G = sb("G", [P, NSUB, 8], F32)
        E1 = sb("E1", [P, NSUB, 4], F32)
        RR = sb("RR", [P, NSUB, 4], F32)
        CC = sb("CC", [P, NSUB, 4], F32)
        EG = sb("EG", [P, NSUB, 4], F32)
        SM = sb("SM", [P, 8, 4], F32)
        BNS = sb("BNS", [P, 4, 6], F32)
        MV = sb("MV", [P, 4, 2], F32)
        HST = sb("HST", [P, 8], F32)
        CL = sb("CL", [P, NCOL], F32)
        CD = sb("CD", [P, 32], F32)
        BG = sb("BG", [P, NSUB, 8], F32)
        WQKV = sb("WQKV", [P, 3, 8, 128], BF16)
        WRG = sb("WRG", [P, 2, 4, 2, 256], BF16)
        WGT = sb("WGT", [P, 24, 8], BF16)
        IDENT = sb("IDENT", [P, P], F32)
        UF = sb("UF", [P, P], F32)
        ONESF = sb("ONESF", [P, P], F32)
        ONESB = sb("ONESB", [P, P], BF16)
        PSB = [st.enter_context(nc.psum_tensor("ps%d" % b, [P, T], F32)) for b in range(8)]


        def body(order_in):
            dry = order_in is None
            S = Sched(nc, st, dry=dry)
            col = lambda name, i=0: CL[:, COLS[name] + i:COLS[name] + i + 1]
            dumps = []

            def dump(name, ap, shape, keys):
                if not dbg or dry or name in dumps:
                    return
                dumps.append(name)
                d = nc.dram_tensor("dbg_" + name, list(shape), F32, kind="ExternalOutput").ap()
                S.dma('sp', d, ap, 'dbg_' + name, reads=keys)

            pstate = {"b": 0, "qb": None, "q": 4}

            def ps_full(hold=False):
                b = pstate["b"]
                while b in pstate.setdefault("held", set()):
                    b = (b + 1) % 8
                pstate["b"] = (b + 1) % 8
                if hold:
                    pstate["held"].add(b)
                return PSB[b], [("ps", b)]

            def ps_release(keys):
                pstate["held"].discard(keys[0][1])

            def ps_q():
                t_, k_ = ps_full()
                return t_[:, 0:128], k_

            def act(out, in_, func, reads, writes, scale=1.0, bias=0.0, accum=None, eng='act'):
                if accum is None:
                    S.op('act', lambda E: E.activation(out=out, in_=in_, func=func, bias=bias, scale=scale),
                         reads, writes)
                else:
                    S.op('act', lambda E: E.activation(out=out, in_=in_, func=func, bias=bias, scale=scale,
                                                       accum_out=accum), reads, writes)

            def tt(eng, out, a, b, op, reads, writes):
                S.op(eng, lambda E: E.tensor_tensor(out=out, in0=a, in1=b, op=op), reads, writes)

            def ts(eng, out, a, s1, s2, op0, op1, reads, writes):
                if s2 is None:
                    S.op(eng, lambda E: E.tensor_scalar(out=out, in0=a, scalar1=s1, scalar2=None, op0=op0), reads, writes)
                else:
                    S.op(eng, lambda E: E.tensor_scalar(out=out, in0=a, scalar1=s1, scalar2=s2, op0=op0, op1=op1),
                         reads, writes)

            def stt(out, a, scalar, b, op0, op1, reads, writes):
                S.op('dve', lambda E: E.scalar_tensor_tensor(out=out, in0=a, scalar=scalar, in1=b, op0=op0, op1=op1),
                     reads, writes)

            def cp(eng, out, in_, reads, writes):
                if eng == 'act':
                    S.op('act', lambda E: E.activation(out=out, in_=in_, func=AF.Copy), reads, writes)
                else:
                    S.op(eng, lambda E: E.tensor_copy(out=out, in_=in_), reads, writes)

            def mm(out, pairs, reads, writes):
                n = len(pairs)
                ths = []
                for i, (l, r) in enumerate(pairs):
                    ths.append(lambda E, l=l, r=r, i=i: E.matmul(out, lhsT=l, rhs=r, start=(i == 0), stop=(i == n - 1)))
                S.op('pe', ths, reads, writes)

            def mm_seq(out, pairs, per_reads, common, writes):
                n = len(pairs)
                for i, ((l, r), rk) in enumerate(zip(pairs, per_reads)):
                    S.op('pe', lambda E, l=l, r=r, i=i: E.matmul(out, lhsT=l, rhs=r, start=(i == 0), stop=(i == n - 1)),
                         list(rk) + list(common), writes)

            def sigmoid_recip(dst, src, reads_src, kdst, scale=-1.0, bias=0.0):
                act(dst, src, AF.Exp, reads_src, kdst, scale=scale, bias=bias)
                ts('dve', dst, dst, 1.0, None, ALU.add, None, kdst, kdst)
                S.op('dve', lambda E: E.reciprocal(out=dst, in_=dst), kdst, kdst)

            def sigmoid_chain(dst, src, reads_src, kdst, scale=-1.0, bias=0.0):
                act(dst, src, AF.Exp, reads_src, kdst, scale=scale, bias=bias)
                act(dst, dst, AF.Ln, kdst, kdst, scale=1.0, bias=1.0)
                act(dst, dst, AF.Exp, kdst, kdst, scale=-1.0)

            S.dma('sp', CL[:], d_cols, 'ld0', writes=['CL'])
            S.dma('sp', BG[:].rearrange("p a b -> p (a b)"), d_bg, 'ld0', writes=['BG'])
            S.dma('pool', WQKV[:].rearrange("p a b c -> p (a b c)"), d_wqkv, 'ldp', writes=['WQKV'])
            S.dma('pool', WRG[:].rearrange("p a b c d -> p (a b c d)"), d_wrg, 'ldp', writes=['WRG'])
            S.dma('pool', WGT[:].rearrange("p a b -> p (a b)"), d_wgt, 'ldp', writes=['WGT'])
            S.fix_total('ld0')
            S.fix_total('ldp')
            NCAST = 6
            cst = {"i": 0}
            early = [("gu1", u) for u in range(11)] + [("d1", u) for u in range(8)] + [("win", u) for u in (0, 1, 4, 5)]
            late = ([("win", 2), ("win", 3), ("woutm", 0), ("woutm", 1), ("win", 6), ("win", 7), ("woutr", 0), ("woutr", 1)]
                    + [("gu2", u) for u in range(11)] + [("d2", u) for u in range(8)])
            if n_pre == 0:
                early, late = early + late, []

            def emit_casts(lst, reads=()):
                for k, u in lst:
                    ci = cst["i"]
                    S.dma('pool', d_ws[k][u], d_w[k][u], 'cast%d' % (ci % NCAST), reads=list(reads),
                          writes=[("ws", k, u), ("castslot", ci % NCAST)])
                    cst["i"] += 1

            emit_casts(early)
            S.op('pool', lambda E: E.memset(ONESF[:], 1.0), writes=['ONESF'])
            S.op('pool', lambda E: E.memset(ONESB[:], 1.0), writes=['ONESB'])
            S.op('pool', lambda E: E.memset(IDENT[:], 1.0), writes=['IDENT'])
            S.op('pool', lambda E: E.affine_select(out=IDENT[:], in_=IDENT[:], pattern=[[-1, P]], compare_op=ALU.is_equal,
                                                   fill=0.0, base=0, channel_multiplier=1), ['IDENT'], ['IDENT'])
            S.op('pool', lambda E: E.memset(UF[:], 1.0), writes=['UF'])
            S.op('pool', lambda E: E.affine_select(out=UF[:], in_=UF[:], pattern=[[1, P]], compare_op=ALU.is_ge,
                                                   fill=0.0, base=0, channel_multiplier=-1), ['UF'], ['UF'])
            S.op('pool', lambda E: E.memset(ST[:].rearrange("p a b c -> p (a b c)"), 0.0), writes=[("ST", h, e) for h in range(4) for e in range(2)])
            S.op('pool', lambda E: E.memset(STB[:].rearrange("p a b c -> p (a b c)"), 0.0), writes=[("STB", h, e) for h in range(4) for e in range(2)])
            S.op('pool', lambda E: E.memset(HALO[:].rearrange("p a b c -> p (a b c)"), 0.0), writes=[("HALO", k, c) for k in range(2) for c in range(8)])
            S.op('pool', lambda E: E.memset(HST[:], 0.0), writes=[("HST", c) for c in range(8)])
            S.op('pool', lambda E: E.memset(VTOK[:].rearrange("p a b c -> p (a b c)"), 0.0), writes=['VTOK0', 'VTOK1'])
            lam = CL[:, COLS["rlam"]:COLS["rlam"] + 8]
            act(CD[:, 0:8], lam, AF.Exp, ['CL'], ['CD'], scale=-1.0)
            act(CD[:, 0:8], CD[:, 0:8], AF.Ln, ['CD'], ['CD'], bias=1.0)
            ts('dve', CD[:, 8:16], CD[:, 0:8], -16.0, None, ALU.mult, None, ['CD'], ['CD'])
            ts('dve', CD[:, 0:8], CD[:, 0:8], -8.0, None, ALU.mult, None, ['CD'], ['CD'])
            ts('dve', CD[:, 16:24], CL[:, COLS["rba"]:COLS["rba"] + 8], -1.0, None, ALU.mult, None, ['CL', 'CD'], ['CD'])
            ts('dve', CD[:, 24:32], CL[:, COLS["rbx"]:COLS["rbx"] + 8], -1.0, None, ALU.mult, None, ['CL', 'CD'], ['CD'])

            units = order_in if order_in is not None else []
            order_rec = []
            wst = {"issued": 0, "consumed": 0, "free": list(range(NS)), "slot_of": {}}

            def prefetch():
                if dry:
                    return
                while wst["issued"] < len(units) and wst["free"]:
                    i = wst["issued"]
                    k, u = units[i]
                    L = wshapes[k][1]
                    s = wst["free"].pop(0)
                    wst["slot_of"][i] = s
                    S.dma('sp', WR[:, s, 0:L], d_ws[k][u], 'wr%d' % s, reads=[("ws", k, u)], writes=[("WR", s)])
                    wst["issued"] += 1

            def consume(kind, u):
                i = wst["consumed"]
                order_rec.append((kind, u))
                wst["consumed"] += 1
                if dry:
                    return i % NS
                assert units[i] == (kind, u), (units[i], kind, u)
                assert wst["issued"] > i, "unit not prefetched (ring too small for this consumption pattern)"
                return wst["slot_of"][i]

            def release(slots):
                if dry:
                    return
                for sl in slots:
                    assert sl not in wst["free"]
                    wst["free"].append(sl)
                prefetch()

            prefetch()

            def sumsq(src, skey, src_all):
                pn, pk = ps_full()
                if src_all is not None:
                    allk = [k for c in range(8) for k in skey(c)]
                    sqk = [("HID", i) for i in range(8)]
                    act(HID[:, 0:8, :].rearrange("p a b -> p (a b)"), src_all, AF.Square, allk, sqk)
                    mm(pn[:], [(ONESB[:], HID[:, c, :]) for c in range(8)], sqk + ['ONESB'], pk)
                else:
                    for c in range(8):
                        b = c % 2
                        act(SQ[:, b, :], src(c), AF.Square, skey(c), [("SQ", b)])
                        S.op('pe', lambda E, b=b, c=c: E.matmul(pn[:], lhsT=ONESB[:], rhs=SQ[:, b, :], start=(c == 0),
                                                              stop=(c == 7)), [("SQ", b), 'ONESB'], pk)
                return pn, pk

            def rmsnorm(src, skey, gname, dst, dkey, nrm_scale=1.0 / D_MODEL, src_all=None):
                pn, pk = sumsq(src, skey, src_all)
                act(RB[:], pn[:], AF.Ln, pk, ['RB'], scale=nrm_scale, bias=EPS)
                act(RB[:], RB[:], AF.Exp, ['RB'], ['RB'], scale=-0.5)
                for c in range(8):
                    stt(dst(c), src(c), col(gname, c), RB[:], ALU.mult, ALU.mult, skey(c) + ['RB', 'CL'], dkey(c))

            Xall = lambda: XB[:, cur["p"]].rearrange("p a b -> p (a b)")
            Xc = lambda c: XB[:, cur["p"], c, :]
            Xk = lambda c: [("X", cur["p"], c)]
            XNc = lambda c: XN[:, c, :]
            XNk = lambda c: [("XN", c)]
            XN_all = [("XN", c) for c in range(8)]
            XNFc = lambda c: XNF[:, c, :]
            XNFk = lambda c: [("XNF", c)]
            XNF_all = [("XNF", c) for c in range(8)]

            def ffn_gen(gu, dd, par):
                Xc_ = lambda c: XB[:, par, c, :]
                Xk_ = lambda c: [("X", par, c)]
                for j in range(NFF):
                    u, jj = divmod(j, 2)
                    if jj == 0:
                        slot = consume(gu, u)
                    wv = WR[:, slot, :].rearrange("p (a b c d) -> p a b c d", a=2, b=2, c=8)
                    pg, pgk = ps_full()
                    pu, puk = ps_full()
                    if j == 0:
                        mm_seq(pg[:], [(wv[:, jj, 0, kc, :], XNF[:, kc, :]) for kc in range(8)],
                               [[("XNF", kc)] for kc in range(8)], [("WR", slot)], pgk)
                    else:
                        mm(pg[:], [(wv[:, jj, 0, kc, :], XNF[:, kc, :]) for kc in range(8)], XNF_all + [("WR", slot)], pgk)
                    mm(pu[:], [(wv[:, jj, 1, kc, :], XNF[:, kc, :]) for kc in range(8)], XNF_all + [("WR", slot)], puk)
                    if jj == 1:
                        release([slot])
                    a = j % 2
                    ga, gb = GT[:, a, :], GT[:, 2 + a, :]
                    sigmoid_chain(ga, pg[:], pgk, [("GT", a)])
                    tt('dve', gb, ga, pg[:], ALU.mult, [("GT", a)] + pgk, [("GT", 2 + a)])
                    tt('dve', HID[:, j, :], gb, pu[:], ALU.mult, [("GT", 2 + a)] + puk, [("HID", j)])
                    yield
                for oc in range(8):
                    slot = consume(dd, oc)
                    wv = WR[:, slot, 0:NFF * 128].rearrange("p (j m) -> p j m", j=NFF)
                    po, pok = ps_full()
                    if oc == 0:
                        mm_seq(po[:], [(wv[:, j, :], HID[:, j, :]) for j in range(NFF)],
                               [[("HID", j)] for j in range(NFF)], [("WR", slot)], pok)
                    else:
                        mm(po[:], [(wv[:, j, :], HID[:, j, :]) for j in range(NFF)],
                           [("HID", j) for j in range(NFF)] + [("WR", slot)], pok)
                    release([slot])
                    stt(Xc_(oc), po[:], 0.5, Xc_(oc), ALU.mult, ALU.add, pok + Xk_(oc), Xk_(oc))
                    yield

            def ffn(gu, dd):
                for _ in ffn_gen(gu, dd, cur["p"]):
                    pass

            pumpst = {"it": None}

            def pump(n):
                it = pumpst["it"]
                if it is None:
                    return
                for _ in range(n):
                    if next(it, "end") == "end":
                        pumpst["it"] = None
                        return

            def win_chunk(m, slotmap):
                u, mmi = divmod(m, 4)
                slot = slotmap[u]
                wv = WR[:, slot, :].rearrange("p (a c d) -> p a c d", a=4, c=8)
                pp, ppk = ps_full()
                if m in (0, 16):
                    mm_seq(pp[:], [(wv[:, mmi, kc, :], XN[:, kc, :]) for kc in range(8)],
                           [[("XN", kc)] for kc in range(8)], [("WR", slot)], ppk)
                else:
                    mm(pp[:], [(wv[:, mmi, kc, :], XN[:, kc, :]) for kc in range(8)], XN_all + [("WR", slot)], ppk)
                return pp, ppk

            def pipelined(n, stages, width=2, npump=0):
                for i0 in range(0, n, width):
                    ctxs = [dict(c=i) for i in range(i0, min(n, i0 + width))]
                    for stg in stages:
                        for cx in ctxs:
                            stg(cx)
                    pump(npump)

            def conv_a(pp, ppk, kind, c, b):
                cp('act', XH[:, b, 3:T + 3], pp[:], ppk, [("XH", b)])
                cp('pool', XH[:, b, 0:3], HALO[:, kind, c, :], [("HALO", kind, c)], [("XH", b)])

            def conv_b(kind, c, wname, bname, dst, dkey, b):
                ts('dve', dst, XH[:, b, 0:T], col(wname, 0 * 8 + c), col(bname, c), ALU.mult, ALU.add,
                   [("XH", b), 'CL'], dkey)
                for tap in range(1, 4):
                    stt(dst, XH[:, b, tap:tap + T], col(wname, tap * 8 + c), dst, ALU.mult, ALU.add,
                        [("XH", b), 'CL'] + dkey, dkey)
                cp('pool', HALO[:, kind, c, :], XH[:, b, T:T + 3], [("XH", b)], [("HALO", kind, c)])

            def mlstm_front(slotmap, full=True):
                def f0(cx):
                    c = cx["c"]; b = c % 2
                    cx["cv"], cx["cvk"] = GT[:, 4 + b, :], [("GT", 4 + b)]
                    cx["sg"], cx["sgk"] = GT[:, 6 + b, :], [("GT", 6 + b)]
                    pp, ppk = win_chunk(c, slotmap)
                    conv_a(pp, ppk, 0, c, b)

                def f1(cx):
                    c = cx["c"]; b = c % 2
                    conv_b(0, c, "mcw", "mcb", cx["cv"], cx["cvk"], b)
                    cp('act', MIX[:, c, :], XH[:, b, 3:T + 3], [("XH", b)], [("MIX", c)])

                def f2(cx):
                    sigmoid_chain(cx["sg"], cx["cv"], cx["cvk"], cx["sgk"])

                def f3(cx):
                    c = cx["c"]
                    cv, cvk, sg, sgk = cx["cv"], cx["cvk"], cx["sg"], cx["sgk"]
                    tt('dve', MIX[:, 8 + c, :], cv, sg, ALU.mult, cvk + sgk, [("MIX", 8 + c)])
                    if full:
                        stt(B2[:, c, :], cv, col("mskip", c), sg, ALU.mult, ALU.mult, cvk + sgk + ['CL'], [("B2", c)])

                pipelined(8, [f0, f1, f2, f3], npump=0 if full else 2)
                release([slotmap[0], slotmap[1]])
                if full:
                    for w, src0 in ((0, 8), (1, 8), (2, 0)):
                        for c in range(8):
                            pp, ppk = ps_full()
                            mm(pp[:], [(WQKV[:, w, c, :], MIX[:, src0 + c, :])], [("MIX", src0 + c), 'WQKV'], ppk)
                            cp('act' if (c % 2) else 'dve', HID[:, w * 8 + c, :], pp[:], ppk, [("HID", w * 8 + c)])
                    return None
                qslot = lambda i: (B2B[:, i // 2, (i % 2) * T:(i % 2 + 1) * T], [("B2", i // 2)])
                pg, pgk = ps_full(hold=True)
                for rnd, ws in ((0, (0, 1)), (1, (2,))):
                    for wi, w in enumerate(ws):
                        src0 = 0 if w == 2 else 8
                        for c in range(8):
                            pp, ppk = ps_full()
                            mm(pp[:], [(WQKV[:, w, c, :], MIX[:, src0 + c, :])], [("MIX", src0 + c), 'WQKV'], ppk)
                            qa, qk = qslot(wi * 8 + c)
                            cp('act' if (c % 2) else 'dve', qa, pp[:], ppk, qk)
                    pump(2)
                    nsl = 8 * len(ws)
                    for s in range(NSUB):
                        ths = []
                        rk = []
                        for i in range(nsl):
                            qa, qk = qslot(i)
                            gi = i if rnd == 0 else 16 + i
                            first = (rnd == 0 and s == 0 and i == 0)
                            last = (rnd == 1 and s == NSUB - 1 and i == nsl - 1)
                            ths.append(lambda E, qa=qa, gi=gi, first=first, last=last, s=s: E.matmul(
                                pg[:, s * 8:(s + 1) * 8], lhsT=qa[:, s * 128:(s + 1) * 128], rhs=WGT[:, gi, :],
                                start=first, stop=last))
                            rk += qk
                        S.op('pe', ths, rk + ['WGT'], pgk)
                ps_release(pgk)
                return pg, pgk

            def mlstm_gates(pre=None):
                if pre is not None:
                    pg, pgk = pre
                else:
                    pg, pgk = ps_q()
                    for s in range(NSUB):
                        mm(pg[:, s * 8:(s + 1) * 8],
                           [(HID[:, i, s * 128:(s + 1) * 128], WGT[:, i, :]) for i in range(24)],
                           [("HID", i) for i in range(24)] + ['WGT'], pgk)
                Gf = G[:].rearrange("p a b -> p (a b)")
                tt('dve', Gf, pg[:, 0:32], BG[:].rearrange("p a b -> p (a b)"), ALU.add, pgk + ['BG'], ['G'])
                act(E1[:], G[:, :, 4:8], AF.Exp, ['G'], ['E1'], scale=-1.0)
                act(E1[:], E1[:], AF.Ln, ['E1'], ['E1'], bias=1.0)
                pb, pbk = ps_q()
                S.tag = S.tag + "F"
                for s in range(NSUB):
                    mm(pb[:, s * 4:(s + 1) * 4], [(UF[:], E1[:, s, :])], ['UF', 'E1'], pbk)
                for s in range(NSUB):
                    mm(pb[:, 16 + s * 4:16 + (s + 1) * 4], [(ONESF[:], E1[:, s, :])], ['ONESF', 'E1'], pbk)
                RRf = RR[:].rearrange("p a b -> p (a b)")
                CCf = CC[:].rearrange("p a b -> p (a b)")
                EGf = EG[:].rearrange("p a b -> p (a b)")
                act(RRf, pb[:, 0:16], AF.Exp, pbk, ['RR'], scale=-1.0)
                tt('dve', CC[:], G[:, :, 0:4], pb[:, 0:16].rearrange("p (a b) -> p a b", a=NSUB), ALU.add,
                   ['G'] + pbk, ['CC'])
                act(CCf, CCf, AF.Exp, ['CC'], ['CC'])
                act(EGf, pb[:, 16:32], AF.Exp, pbk, ['EG'], scale=-1.0)

            def mlstm_tok(s):
                kb = s % 2
                cols_s = slice(s * 128, (s + 1) * 128)
                pk0, pk0k = ps_full()
                pk1, pk1k = ps_full()
                pv0, pv0k = ps_full()
                pv1, pv1k = ps_full()
                for c in range(8):
                    pk_, pkk = (pk0, pk0k) if c < 4 else (pk1, pk1k)
                    mm(pk_[:, (c % 4) * 128:(c % 4 + 1) * 128], [(MIX[:, 8 + c, cols_s], WQKV[:, 1, c, :])],
                       [("MIX", 8 + c), 'WQKV'], pkk)
                for c in range(8):
                    pv_, pvk = (pv0, pv0k) if c < 4 else (pv1, pv1k)
                    mm(pv_[:, (c % 4) * 128:(c % 4 + 1) * 128], [(MIX[:, c, cols_s], WQKV[:, 2, c, :])],
                       [("MIX", c), 'WQKV'], pvk)
                kk = [("KTOK", kb)]
                act(KTOK[:, kb, 0:512], pk0[:], AF.Copy, pk0k, kk, scale=1.0 / 16.0)
                ts('dve', KTOK[:, kb, 512:1024], pk1[:], 1.0 / 16.0, None, ALU.mult, None, pk1k, kk)
                vk = ['VTOK%d' % kb]
                for h in range(4):
                    pv_, pvk = (pv0, pv0k) if h < 2 else (pv1, pv1k)
                    act(VTOK[:, kb, h, 0:256], pv_[:, (h % 2) * 256:(h % 2 + 1) * 256], AF.Copy, pvk + ['CC'], vk,
                        scale=CC[:, s, h:h + 1])
                cp('dve', VTOK[:, kb, :, 256], CC[:, s, :], ['CC'], vk)

            def mlstm_state(s, shadow=True):
                kb = s % 2
                for h in range(4):
                    for e in range(2):
                        pu, puk = ps_full()
                        mm(pu[:, 0:257], [(KTOK[:, kb, (2 * h + e) * 128:(2 * h + e + 1) * 128], VTOK[:, kb, h, 0:257])],
                           [("KTOK", kb), 'VTOK%d' % kb], puk)
                        sk = [("ST", h, e)]
                        ts('dve', ST[:, h, e, 0:257], ST[:, h, e, 0:257], EG[:, s, h:h + 1], None, ALU.mult, None,
                           sk + ['EG'], sk)
                        stt(ST[:, h, e, 0:257], pu[:, 0:257], EG[:, s, h:h + 1], ST[:, h, e, 0:257], ALU.mult, ALU.add,
                            puk + sk + ['EG'], sk)
                        if shadow:
                            cp('act', STB[:, h, e, 0:257], ST[:, h, e, 0:257], sk, [("STB", h, e)])

            def mlstm_core(s):
                kb = s % 2
                cols_s = slice(s * 128, (s + 1) * 128)
                pS, pSk = ps_full()
                for h in range(4):
                    mm(pS[:, h * 128:(h + 1) * 128],
                       [(HID[:, 8 + 2 * h + e, cols_s], HID[:, 2 * h + e, cols_s]) for e in range(2)],
                       [("HID", 8 + 2 * h + e) for e in range(2)] + [("HID", 2 * h + e) for e in range(2)], pSk)
                for h in range(4):
                    stt(PT[:, h, :], pS[:, h * 128:(h + 1) * 128], 1.0 / 16.0, UF[:], ALU.mult, ALU.mult,
                        pSk + ['UF'], [("PT", h)])
                for h in range(4):
                    pn, pnk = ps_full()
                    mm(pn[:, 0:257],
                       [(PT[:, h, :], VTOK[:, kb, h, 0:257])] +
                       [(HID[:, 2 * h + e, cols_s], STB[:, h, e, 0:257]) for e in range(2)],
                       [("PT", h), 'VTOK%d' % kb] + [("HID", 2 * h + e) for e in range(2)] +
                       [("STB", h, e) for e in range(2)], pnk)
                    cp('act', NUMS[:, h, 0:257], pn[:, 0:257], pnk, [("NUMS", h)])

            def mlstm_post(s):
                cols_s = slice(s * 128, (s + 1) * 128)
                nk = [("NUMS", h) for h in range(4)]
                tt('dve', SM[:, 0, :], NUMS[:, :, 256], RR[:, s, :], ALU.mult, nk + ['RR'], ['SM0'])
                act(SM[:, 0, :], SM[:, 0, :], AF.Abs, ['SM0'], ['SM0'])
                ts('dve', SM[:, 0, :], SM[:, 0, :], 1.0, None, ALU.max, None, ['SM0'], ['SM0'])
                S.op('dve', lambda E: E.reciprocal(out=SM[:, 0, :], in_=SM[:, 0, :]), ['SM0'], ['SM0'])
                tt('dve', SM[:, 1, :], SM[:, 0, :], RR[:, s, :], ALU.mult, ['SM0', 'RR'], ['SM1'])
                for h in range(4):
                    hk = [("GT", 8 + h // 2)]
                    act(HS[:, h * 256:(h + 1) * 256], NUMS[:, h, 0:256], AF.Copy, [("NUMS", h), 'SM1'], hk,
                        scale=SM[:, 1, h:h + 1])
                    S.op('dve', lambda E, h=h: E.bn_stats(out=BNS[:, h, :], in_=HS[:, h * 256:(h + 1) * 256]), hk, [("BNS", h)])
                    S.op('dve', lambda E, h=h: E.bn_aggr(out=MV[:, h, :], in_=BNS[:, h, :]), [("BNS", h)], [("MV", h)])
                mvk = [("MV", h) for h in range(4)]
                act(SM[:, 2, :], MV[:, :, 1], AF.Ln, mvk, ['SM2'], bias=EPS)
                act(SM[:, 2, :], SM[:, 2, :], AF.Exp, ['SM2'], ['SM2'], scale=-0.5)
                for h in range(4):
                    ts('dve', HN[:, h * 256:(h + 1) * 256], HS[:, h * 256:(h + 1) * 256], MV[:, h, 0:1], SM[:, 2, h:h + 1],
                       ALU.subtract, ALU.mult, [("GT", 8 + h // 2), ("MV", h), 'SM2'], [("GT", 6 + h // 2)])

            def mlstm_postT(s):
                cols_s = slice(s * 128, (s + 1) * 128)
                for half in range(2):
                    ptb, ptk = ps_full()
                    for cc in range(4):
                        c = half * 4 + cc
                        S.op('pe', lambda E, c=c, cc=cc, ptb=ptb: E.transpose(ptb[:, cc * 128:(cc + 1) * 128],
                                                                             HN[:, c * 128:(c + 1) * 128], IDENT[:]),
                             [("GT", 6 + c // 4), 'IDENT'], ptk)
                    for cc in range(4):
                        c = half * 4 + cc
                        stt(B2[:, c, cols_s], ptb[:, cc * 128:(cc + 1) * 128], col("mln", c), B2[:, c, cols_s],
                            ALU.mult, ALU.add, ptk + [("B2", c), 'CL'], [("B2", c)])

            def mlstm_finish(slotmap, before_b2=None):
                def zpart(c):
                    a = c % 2
                    pp, ppk = win_chunk(8 + c, slotmap)
                    sg, sgk = GT[:, 4 + a, :], [("GT", 4 + a)]
                    sigmoid_chain(sg, pp[:], ppk, sgk)
                    tt('dve', sg, sg, pp[:], ALU.mult, sgk + ppk, sgk)
                    return sg, sgk

                def bpart(c, sg, sgk):
                    tt('dve', B2[:, c, :], B2[:, c, :], sg, ALU.mult, [("B2", c)] + sgk, [("B2", c)])

                z0 = zpart(0)
                z1 = zpart(1)
                if before_b2 is not None:
                    before_b2()
                bpart(0, *z0)
                bpart(1, *z1)
                for c in range(2, 8):
                    bpart(c, *zpart(c))
                release([slotmap[2], slotmap[3]])
                dump("out_m", B2[:], [P, 8, T], [("B2", c) for c in range(8)])
                rmsnorm(lambda c: B2[:, c, :], lambda c: [("B2", c)], "gom",
                        lambda c: MIX[:, c, :], lambda c: [("MIX", c)], src_all=B2[:].rearrange("p a b -> p (a b)"))

            B3c = lambda c: HIDF[:, 8 + 2 * c:8 + 2 * c + 2, :].rearrange("p a b -> p (a b)")
            B3k = lambda c: [("HID", 8 + 2 * c), ("HID", 8 + 2 * c + 1)]

            def rglru(slotmap, full):
                def c0(cx):
                    c = cx["c"]
                    pp, ppk = win_chunk(16 + c, slotmap)
                    conv_a(pp, ppk, 1, c, c % 2)

                if full:
                    xcrb = lambda c: (HID[:, c, :], ("HID", c))
                    tslots = [[(GT[:, 5 * pp_ + i, :], [("GT", 5 * pp_ + i)]) for i in range(5)] for pp_ in range(2)]
                else:
                    xcrb = lambda c: (MIX[:, c, :], ("MIX", c))
                    nk_ = [("NUMS", h) for h in range(4)]
                    tslots = [[(GT[:, 4 + i, :], [("GT", 4 + i)]) for i in range(5)],
                              [(GT[:, 9, :], [("GT", 9)]), (NUMSF[:, 0:T], nk_), (NUMSF[:, T:2 * T], nk_),
                               (KTOKF[:, 0, :], [("KTOK", 0)]), (KTOKF[:, 1, :], [("KTOK", 1)])]]

                def c1(cx):
                    c = cx["c"]
                    conv_b(1, c, "rcw", "rcb", B2[:, c, :], [("B2", c)], c % 2)
                    cp('act', xcrb(c)[0], B2[:, c, :], [("B2", c)], [xcrb(c)[1]])

                pipelined(8, [c0, c1], npump=0 if full else 1)
                release([slotmap[4], slotmap[5]])
                if full:
                    S.tag = S.tag.replace("rglru", "woutm")
                    wout_half("woutm", 0)
                    S.tag = S.tag.replace("woutm", "rglru")
                    slotmap[6] = consume("win", 6)
                    slotmap[7] = consume("win", 7)
                def g0(cx):
                    c = cx["c"]; n = c // 2; oc = (c % 2) * 128
                    sl = tslots[c % 2]
                    cx["g"] = lambda i, sl=sl: sl[i][0]
                    cx["gk"] = lambda i, sl=sl: list(sl[i][1])
                    cx["pa"], cx["pak"] = ps_full()
                    cx["px"], cx["pxk"] = ps_full()
                    rd = [xcrb(2 * n)[1], xcrb(2 * n + 1)[1], 'WRG']
                    mm(cx["pa"][:], [(WRG[:, 0, n, kc, oc:oc + 128], xcrb(2 * n + kc)[0]) for kc in range(2)], rd, cx["pak"])
                    mm(cx["px"][:], [(WRG[:, 1, n, kc, oc:oc + 128], xcrb(2 * n + kc)[0]) for kc in range(2)], rd, cx["pxk"])
                    if full:
                        cx["py"], cx["pyk"] = win_chunk(24 + c, slotmap)

                sig = sigmoid_chain

                def gA(cx):
                    c = cx["c"]
                    if full:
                        act(B3c(c), cx["py"][:], AF.Square, cx["pyk"], B3k(c), scale=0.044715 ** 0.5)
                        stt(B3c(c), B3c(c), 1.0, cx["py"][:], ALU.add, ALU.mult, B3k(c) + cx["pyk"], B3k(c))

                def gB(cx):
                    c = cx["c"]
                    if full:
                        sigmoid_chain(B3c(c), B3c(c), B3k(c), B3k(c), scale=-GELU_C)

                def gC(cx):
                    c = cx["c"]
                    if full:
                        tt('dve', B3c(c), B3c(c), cx["py"][:], ALU.mult, B3k(c) + cx["pyk"], B3k(c))

                def g1(cx):
                    c, g, gk = cx["c"], cx["g"], cx["gk"]
                    sig(g(0), cx["pa"][:], cx["pak"] + ['CD'], gk(0), scale=-1.0, bias=CD[:, 16 + c:17 + c])

                def g2(cx):
                    c, g, gk = cx["c"], cx["g"], cx["gk"]
                    sig(g(1), cx["px"][:], cx["pxk"] + ['CD'], gk(1), scale=-1.0, bias=CD[:, 24 + c:25 + c])

                def g3(cx):
                    c, g, gk = cx["c"], cx["g"], cx["gk"]
                    act(g(2), g(0), AF.Exp, gk(0) + ['CD'], gk(2), scale=CD[:, c:c + 1])
                    act(g(3), g(0), AF.Exp, gk(0) + ['CD'], gk(3), scale=CD[:, 8 + c:9 + c])
                    act(g(3), g(3), AF.Ln, gk(3), gk(3), scale=-1.0, bias=1.0)
                    act(g(3), g(3), AF.Exp, gk(3), gk(3), scale=0.5)
                    tt('pool', g(1), g(1), B2[:, c, :], ALU.mult, gk(1) + [("B2", c)], gk(1))

                def g5(cx):
                    c, g, gk = cx["c"], cx["g"], cx["gk"]
                    tt('dve', g(1), g(1), g(3), ALU.mult, gk(1) + gk(3), gk(1))
                    S.op('dve', lambda E, c=c, o=g(4), d0=g(2), d1=g(1): E.tensor_tensor_scan(
                        out=o, data0=d0, data1=d1, initial=HST[:, c:c + 1], op0=ALU.mult, op1=ALU.add),
                         gk(2) + gk(1) + [("HST", c)], gk(4))
                    cp('pool', HST[:, c:c + 1], g(4)[:, T - 1:T], gk(4), [("HST", c)])

                def g7(cx):
                    c, g, gk = cx["c"], cx["g"], cx["gk"]
                    if full:
                        tt('pool', B3c(c), B3c(c), g(4), ALU.mult, gk(4) + B3k(c), B3k(c))

                pipelined(8, [g0, gA, gB, gC, g1, g2, g3, g5, g7], npump=0 if full else 2)
                if full:
                    release([slotmap[6], slotmap[7]])
                if full:
                    dump("out_r", HIDF[:, 8:24, :], [P, 16, 256], [("HID", i) for i in range(8, 24)])
                    dump("xcr", B2[:], [P, 8, T], [("B2", c) for c in range(8)])
                    rmsnorm(B3c, B3k, "gor", lambda c: MIX[:, 8 + c, :], lambda c: [("MIX", 8 + c)],
                            src_all=HIDF[:, 8:24, :].rearrange("p a b -> p (a b)"))

            def wout_half(kind, kbase):
                slots = {}
                for oc in range(8):
                    u, oo = divmod(oc, 4)
                    if oo == 0:
                        slots[u] = consume(kind, u)
                    wv = WR[:, slots[u], :].rearrange("p (a c d) -> p a c d", a=4, c=8)
                    po, pok = ps_full()
                    if oc == 0:
                        mm_seq(po[:], [(wv[:, oo, kc, :], MIX[:, kbase + kc, :]) for kc in range(8)],
                               [[("MIX", kbase + kc)] for kc in range(8)], [("WR", slots[u])], pok)
                    else:
                        mm(po[:], [(wv[:, oo, kc, :], MIX[:, kbase + kc, :]) for kc in range(8)],
                           [("MIX", kbase + kc) for kc in range(8)] + [("WR", slots[u])], pok)
                    if oo == 3:
                        release([slots[u]])
                    tt('dve', Xc(oc), po[:], Xc(oc), ALU.add, pok + Xk(oc), Xk(oc))

            def load_x(src, t, par):
                for c in range(8):
                    S.dma('sp', XB[:, par, c, :], src[c * P:(c + 1) * P, t * T:(t + 1) * T], 'ldx%d_%d' % (par, c),
                          writes=[("X", par, c)])

            def apply_flag():
                fl = col("flag")
                for h in range(4):
                    for e in range(2):
                        sk = [("ST", h, e)]
                        ts('dve', ST[:, h, e, 0:257], ST[:, h, e, 0:257], fl, None, ALU.mult, None, sk + ['CL'], sk)
                        cp('pool', STB[:, h, e, 0:257], ST[:, h, e, 0:257], sk, [("STB", h, e)])
                ts('dve', HST[:], HST[:], fl, None, ALU.mult, None, [("HST", c) for c in range(8)] + ['CL'],
                   [("HST", c) for c in range(8)])
                hf = HALO[:].rearrange("p a b c -> p (a b c)")
                hk = [("HALO", k, c) for k in range(2) for c in range(8)]
                ts('dve', hf, hf, fl, None, ALU.mult, None, hk + ['CL'], hk)

            def tile_pass(src, t, full, tile_idx, skip_ffn1=False):
                tg = lambda name: setattr(S, 'tag', "%s%d:%s" % ("M" if full else "P", t, name))
                tg("norm1")
                cur["p"] = tile_idx % 2
                ensure_load(tile_idx)
                ensure_load(tile_idx + 1)
                if not skip_ffn1:
                    rmsnorm(Xc, Xk, "g1", XNFc, XNFk, src_all=Xall())
                    tg("ffn1")
                    ffn("gu1", "d1")
                if not full and late:
                    n_now = 4 if tile_idx + 1 < n_pre else len(late)
                    emit_casts(late[:n_now], reads=Xk(7))
                    del late[:n_now]
                tg("normmix")
                dump("x1", XB[:, cur["p"]], [P, 8, T], [("X", cur["p"], c) for c in range(8)])
                rmsnorm(Xc, Xk, "gmix", XNc, XNk, src_all=Xall())
                slotmap = {}
                if full:
                    slotmap[0] = consume("win", 0)
                    slotmap[1] = consume("win", 1)
                    tg("mfront")
                    mlstm_front(slotmap)
                    dump("xs", B2[:], [P, 8, T], [("B2", c) for c in range(8)])
                    tg("mgates")
                    mlstm_gates()
                    dump("G", G[:], [P, NSUB, 8], ['G'])
                    dump("RR", RR[:], [P, NSUB, 4], ['RR'])
                    dump("CC", CC[:], [P, NSUB, 4], ['CC'])
                    dump("EG", EG[:], [P, NSUB, 4], ['EG'])
                    tg("mtok0")
                    mlstm_tok(0)
                    for s in range(NSUB):
                        tg("mout%d" % s)
                        mlstm_core(s)
                        tg("mstate%d" % s)
                        mlstm_state(s)
                        if s > 0:
                            tg("mpost%d" % (s - 1))
                            mlstm_postT(s - 1)
                        if s + 1 < NSUB:
                            tg("mtok%d" % (s + 1))
                            mlstm_tok(s + 1)
                        tg("mpost%d" % s)
                        mlstm_post(s)
                    dump("hnT", B2[:], [P, 8, T], [("B2", c) for c in range(8)])
                    slotmap[2] = consume("win", 2)
                    slotmap[3] = consume("win", 3)
                    tg("mfinish")
                    mlstm_finish(slotmap, before_b2=lambda: mlstm_postT(NSUB - 1))
                    tg("rglru")
                    slotmap[4] = consume("win", 4)
                    slotmap[5] = consume("win", 5)
                    rglru(slotmap, True)
                    tg("woutr")
                    wout_half("woutr", 8)
                    dump("x2", XB[:, cur["p"]], [P, 8, T], [("X", cur["p"], c) for c in range(8)])
                    tg("norm2")
                    rmsnorm(Xc, Xk, "g2", XNFc, XNFk, src_all=Xall())
                    tg("ffn2")
                    ffn("gu2", "d2")
                    tg("normfin")
                    dump("x3", XB[:, cur["p"]], [P, 8, T], [("X", cur["p"], c) for c in range(8)])
                    ob = lambda c: GT[:, 8 + c % 2, :]
                    obk = lambda c: [("GT", 8 + c % 2)]
                    pn, pk = sumsq(Xc, Xk, Xall())
                    act(RB[:], pn[:], AF.Ln, pk, ['RB'], scale=1.0 / D_MODEL, bias=EPS)
                    act(RB[:], RB[:], AF.Exp, ['RB'], ['RB'], scale=-0.5)
                    for c in range(8):
                        stt(ob(c), Xc(c), col("gfin", c), RB[:], ALU.mult, ALU.mult, Xk(c) + ['RB', 'CL'], obk(c))
                        S.dma('sp', d_out[c * P:(c + 1) * P, t * T:(t + 1) * T], ob(c), 'sto%d' % (c % 2), reads=obk(c))

            def prefix_mixer(t):
                tg = lambda name: setattr(S, 'tag', "P%d:%s" % (t, name))
                slotmap = {}
                slotmap[0] = consume("win", 0)
                slotmap[1] = consume("win", 1)
                tg("mfront")
                pre = mlstm_front(slotmap, False)
                tg("mgates")
                mlstm_gates(pre)
                pump(1)
                tg("mtok0")
                mlstm_tok(0)
                pump(1)
                tg("mtok1")
                mlstm_tok(1)
                pump(1)
                for s in range(NSUB):
                    tg("mstate%d" % s)
                    mlstm_state(s, shadow=False)
                    if s + 2 < NSUB:
                        tg("mtok%d" % (s + 2))
                        mlstm_tok(s + 2)
                    pump(1)
                slotmap[4] = consume("win", 4)
                slotmap[5] = consume("win", 5)
                tg("rglru")
                rglru(slotmap, False)

            passes = [(d_xpT, t, False) for t in range(n_pre)] + [(d_xT, t, True) for t in range(n_main)]
            loaded = set()

            def ensure_load(i):
                if i < len(passes) and i not in loaded:
                    load_x(passes[i][0], passes[i][1], i % 2)
                    loaded.add(i)

            ensure_load(0)
            if n_pre > 0:
                ensure_load(1)
                cur["p"] = 0
                S.tag = "P0:norm1"
                rmsnorm(Xc, Xk, "g1", XNFc, XNFk, src_all=Xall())
                S.tag = "P0:ffn1"
                ffn("gu1", "d1")
                for t in range(n_pre):
                    par = t % 2
                    cur["p"] = par
                    if late:
                        n_now = 4 if t + 1 < n_pre else len(late)
                        emit_casts(late[:n_now], reads=Xk(7))
                        del late[:n_now]
                    S.tag = "P%d:normmix" % t
                    rmsnorm(Xc, Xk, "gmix", XNc, XNk, src_all=Xall())
                    ensure_load(t + 2)
                    if t + 1 < len(passes):
                        ensure_load(t + 1)
                        cur["p"] = 1 - par
                        S.tag = "P%d:norm1" % (t + 1)
                        rmsnorm(Xc, Xk, "g1", XNFc, XNFk, src_all=Xall())
                        cur["p"] = par
                        pumpst["it"] = ffn_gen("gu1", "d1", 1 - par)
                    prefix_mixer(t)
                    pump(10 ** 6)
                if late:
                    emit_casts(late)
                    del late[:]
                apply_flag()
            for i in range(n_pre, len(passes)):
                tile_pass(passes[i][0], passes[i][1], True, i, skip_ffn1=(i == n_pre and n_pre > 0))
            for b in range(2):
                S._wait('sp', ('sto%d' % b, S.cnt['sto%d' % b]))
            for name in dumps:
                S._wait('sp', ('dbg_' + name, S.cnt['dbg_' + name]))
            return S, order_rec

        _, order = body(None)
        S, _ = body(order)
        S.emit()
    nc._pe_tags = S.pe_tags
    return nc


def _cols_of(v):
    return np.ascontiguousarray(v.reshape(-1, P).T)


def _prep_shared(inp):
    f = lambda a: np.asarray(a, dtype=np.float32)
    sh = {}
    cols = np.zeros((P, NCOL), np.float32)

    def put(name, v):
        c = _cols_of(f(v).reshape(-1))
        cols[:, COLS[name]:COLS[name] + c.shape[1]] = c
    put("g1", inp["norm_ffn1"][0]); put("gmix", inp["norm_mix"][0]); put("g2", inp["norm_ffn2"][0])
    put("gfin", inp["norm_final"])
    for tap in range(4):
        cols[:, COLS["mcw"] + tap * 8:COLS["mcw"] + tap * 8 + 8] = _cols_of(f(inp["m_conv_w"][0][tap]))
        cols[:, COLS["rcw"] + tap * 8:COLS["rcw"] + tap * 8 + 8] = _cols_of(f(inp["r_conv_w"][0][tap]))
    put("mcb", inp["m_conv_b"][0]); put("rcb", inp["r_conv_b"][0]); put("rba", inp["r_b_a"][0])
    put("rbx", inp["r_b_x"][0]); put("rlam", inp["r_lam"][0]); put("mln", inp["m_ln_w"][0])
    put("mskip", inp["m_skip"][0]); put("gom", inp["out_norm_m"][0]); put("gor", inp["out_norm_r"][0])
    sh["cols"] = cols
    sh["bgates"] = np.ascontiguousarray(np.broadcast_to(np.tile(f(inp["m_b_gates"][0]), NSUB)[None, :], (P, NSUB * 8)))
    wqkv = np.zeros((P, 3, 8, 128), np.float32)
    for w, name in enumerate(["m_wq", "m_wk", "m_wv"]):
        blk = f(inp[name][0])
        for c in range(8):
            for j in range(32):
                wqkv[4 * j:4 * j + 4, w, c, 4 * j:4 * j + 4] = blk[c * 32 + j]
    sh["wqkv"] = wqkv.reshape(P, -1)
    wrg = np.zeros((P, 2, 4, 2, 256), np.float32)
    for w, name in enumerate(["r_w_a", "r_w_x"]):
        a = f(inp[name][0])
        wrg[:, w] = a.reshape(4, 2, P, 256).transpose(2, 0, 1, 3)
    sh["wrg"] = wrg.reshape(P, -1)
    sh["wgt"] = np.ascontiguousarray(f(inp["m_w_gates"][0]).reshape(24, P, 8).transpose(1, 0, 2)).reshape(P, -1)
    ffn_w = {"1": (inp["ffn1_wg"], inp["ffn1_wu"], inp["ffn1_wd"]), "2": (inp["ffn2_wg"], inp["ffn2_wu"], inp["ffn2_wd"])}
    for i in ("1", "2"):
        wg = f(ffn_w[i][0][0]).reshape(8, P, 11, 2, 128)
        wu = f(ffn_w[i][1][0]).reshape(8, P, 11, 2, 128)
        gu = np.stack([wg, wu], axis=0)
        sh["w_gu" + i] = np.ascontiguousarray(gu.transpose(3, 2, 4, 0, 1, 5)).reshape(11, P, SLOT)
        wd = f(ffn_w[i][2][0]).reshape(NFF, P, 8, 128)
        sh["w_d" + i] = np.ascontiguousarray(wd.transpose(2, 1, 0, 3)).reshape(8, P, NFF * 128)
    win = f(inp["w_in"][0]).reshape(8, P, 8, 4, 128)
    sh["w_win"] = np.ascontiguousarray(win.transpose(2, 1, 3, 0, 4)).reshape(8, P, SLOT)
    wout = f(inp["w_out"][0]).reshape(2, 8, P, 2, 4, 128)
    sh["w_woutm"] = np.ascontiguousarray(wout[0].transpose(2, 1, 3, 0, 4)).reshape(2, P, SLOT)
    sh["w_woutr"] = np.ascontiguousarray(wout[1].transpose(2, 1, 3, 0, 4)).reshape(2, P, SLOT)
    return sh


_CACHE = {}


def kernel(**inputs):
    x = np.asarray(inputs["x"], dtype=np.float32)
    B, SEQ, D = x.shape
    half = SEQ // 2
    n_tiles = half // T
    sh = _prep_shared(inputs)
    key = (n_tiles, half)
    if key not in _CACHE:
        _CACHE[key] = build_program(n_tiles, n_tiles, half, half)
    nc = _CACHE[key]
    in_maps = []
    for core in range(8):
        b, hh = divmod(core, 2)
        m = dict(sh)
        m["xT"] = np.ascontiguousarray(x[b, hh * half:(hh + 1) * half, :].T)
        m["xpT"] = np.ascontiguousarray(x[b, 0:half, :].T)
        cols = sh["cols"].copy()
        cols[:, COLS["flag"]] = float(hh)
        m["cols"] = cols
        for k in ("w_gu1", "w_d1", "w_win", "w_woutm", "w_woutr", "w_gu2", "w_d2"):
            a = sh[k]
            pad = np.full((1, a.shape[2]), float(core), np.float32)
            m[k] = np.concatenate([a.reshape(-1, a.shape[2]), pad], axis=0)
        in_maps.append(m)
    res = run_bass_kernel_spmd(nc, in_maps, core_ids=list(range(8)))
    out = np.empty((B, SEQ, D), np.float32)
    for core in range(8):
        b, hh = divmod(core, 2)
        out[b, hh * half:(hh + 1) * half, :] = res.results[core]["outT"].T
    return out
```
